# Optimizing a Trainium2 kernel written in Bass

```python
import math
import jax, jax.numpy as jnp
from jax import lax
import numpy as np

D_MODEL = 1024
BATCH = 8
SEQ = 2048
DEPTH = 1
DEC_BATCH = 128
DEC_SEQ = 1
PAST_LEN = 16384
PAGE_SIZE = 128

D_CONV = D_MODEL
CONV_A_WIDTH = 31
D_SSM = D_MODEL
SSM_HEAD_DIM = 64
SSM_HEADS = D_SSM // SSM_HEAD_DIM
SSM_GROUPS = 2
D_STATE = 128
SSM_CONV_WIDTH = 4
SSM_CHUNK = 128
D_XBC = D_SSM + 2 * SSM_GROUPS * D_STATE
D_MIX = D_CONV + D_SSM
D_IN = 2 * D_CONV + D_SSM + D_XBC + SSM_HEADS
D_FF = 2816
FFN_RES_WEIGHT = 0.5
NORM_EPS = 1e-5

kernel_name = "hybrid_conformerconv_ssd_macaron_step"


def rms_norm(x, g):
    xf = x.astype(jnp.float32)
    y = xf * lax.rsqrt(jnp.mean(xf * xf, axis=-1, keepdims=True) + NORM_EPS)
    return (y * g.astype(jnp.float32)).astype(x.dtype)


def layer_norm(x, g, b):
    xf = x.astype(jnp.float32)
    mu = jnp.mean(xf, axis=-1, keepdims=True)
    xc = xf - mu
    var = jnp.mean(xc * xc, axis=-1, keepdims=True)
    y = xc * lax.rsqrt(var + NORM_EPS) * g.astype(jnp.float32) + b.astype(jnp.float32)
    return y.astype(x.dtype)


def grouped_rms_norm(y, g, n_groups):
    b, l, c = y.shape
    yg = y.reshape(b, l, n_groups, c // n_groups)
    yg = yg * lax.rsqrt(jnp.mean(yg * yg, axis=-1, keepdims=True) + NORM_EPS)
    return yg.reshape(b, l, c) * g.astype(jnp.float32)


def swiglu_ffn(x, w_gate, w_up, w_down):
    return (jax.nn.silu(x @ w_gate) * (x @ w_up)) @ w_down


def causal_dwconv(u, buf, w, bias):
    k_minus_1 = buf.shape[1]
    ext = jnp.concatenate([buf.astype(u.dtype), u], axis=1)
    y = lax.conv_general_dilated(
        ext, w[:, None, :].astype(u.dtype), window_strides=(1,), padding="VALID",
        dimension_numbers=("NWC", "WIO", "NWC"), feature_group_count=u.shape[-1])
    new_buf = ext[:, ext.shape[1] - k_minus_1:]
    return y + bias.astype(u.dtype), new_buf


def ssd_scan(x, dt, A, Bm, Cm, h0):
    b, l, H, P = x.shape
    G, N = Bm.shape[2], Bm.shape[3]
    R = H // G
    Q = min(SSM_CHUNK, l)
    pad = (-l) % Q
    if pad:
        padf = lambda a: jnp.pad(a, [(0, 0), (0, pad)] + [(0, 0)] * (a.ndim - 2))
        x, dt, Bm, Cm = padf(x), padf(dt), padf(Bm), padf(Cm)
    c = (l + pad) // Q
    X = (x * dt[..., None]).reshape(b, c, Q, G, R, P)
    a = (dt * A).reshape(b, c, Q, G, R)
    a_cs = jnp.cumsum(a, axis=2)
    Bc = Bm.reshape(b, c, Q, G, N)
    Cc = Cm.reshape(b, c, Q, G, N)
    seg = a_cs[:, :, :, None] - a_cs[:, :, None, :]
    causal = jnp.tril(jnp.ones((Q, Q), dtype=bool))[None, None, :, :, None, None]
    Lmat = jnp.exp(jnp.where(causal, seg, -jnp.inf))
    CB = jnp.einsum("bcign,bcjgn->bcijg", Cc, Bc)
    y_diag = jnp.einsum("bcijg,bcijgr,bcjgrp->bcigrp", CB, Lmat, X)
    decay_to_end = jnp.exp(a_cs[:, :, -1:] - a_cs)
    s_local = jnp.einsum("bcjgn,bcjgr,bcjgrp->bcgrpn", Bc, decay_to_end, X)
    chunk_decay = jnp.exp(a_cs[:, :, -1])

    def step(h, inp):
        s, d = inp
        return h * d[..., None, None] + s, h

    h_final, h_prev = lax.scan(step, h0.reshape(b, G, R, P, N),
                               (jnp.swapaxes(s_local, 0, 1), jnp.swapaxes(chunk_decay, 0, 1)))
    h_prev = jnp.swapaxes(h_prev, 0, 1)
    y_off = jnp.einsum("bcign,bcgrpn,bcigr->bcigrp", Cc, h_prev, jnp.exp(a_cs))
    y = (y_diag + y_off).reshape(b, c * Q, H, P)[:, :l]
    return y, h_final.reshape(b, H, P, N)


def decoder_layer(x, conv_a_buf, conv_b_buf, ssm_state, p):
    (ffn1_norm, ffn1_w_gate, ffn1_w_up, ffn1_w_down, mix_norm, w_in,
     conv_dw_w, conv_dw_b, conv_ln_g, conv_ln_b,
     ssm_conv_w, ssm_conv_b, ssm_dt_bias, ssm_a_log, ssm_d, ssm_norm,
     w_out, ffn2_norm, ffn2_w_gate, ffn2_w_up, ffn2_w_down) = p
    x = x + FFN_RES_WEIGHT * swiglu_ffn(rms_norm(x, ffn1_norm), ffn1_w_gate, ffn1_w_up, ffn1_w_down)
    h = rms_norm(x, mix_norm)
    bsz, l, _ = h.shape
    proj = h @ w_in
    cuts = [D_CONV, 2 * D_CONV, 2 * D_CONV + D_SSM, 2 * D_CONV + D_SSM + D_XBC]
    glu_a, glu_b, z, xbc, dt_raw = jnp.split(proj, cuts, axis=-1)
    u = glu_a * jax.nn.sigmoid(glu_b)
    ua, new_conv_a = causal_dwconv(u, conv_a_buf, conv_dw_w, conv_dw_b)
    ya = jax.nn.silu(layer_norm(ua, conv_ln_g, conv_ln_b))
    xbc_c, new_conv_b = causal_dwconv(xbc, conv_b_buf, ssm_conv_w, ssm_conv_b)
    xbc_c = jax.nn.silu(xbc_c).astype(jnp.float32)
    xs, Bm, Cm = jnp.split(xbc_c, [D_SSM, D_SSM + SSM_GROUPS * D_STATE], axis=-1)
    dt = jax.nn.softplus(dt_raw.astype(jnp.float32) + ssm_dt_bias.astype(jnp.float32))
    A = -jnp.exp(ssm_a_log.astype(jnp.float32))
    xs_h = xs.reshape(bsz, l, SSM_HEADS, SSM_HEAD_DIM)
    ys, new_ssm = ssd_scan(xs_h, dt, A,
                           Bm.reshape(bsz, l, SSM_GROUPS, D_STATE),
                           Cm.reshape(bsz, l, SSM_GROUPS, D_STATE),
                           ssm_state.astype(jnp.float32))
    ys = ys + ssm_d.astype(jnp.float32)[:, None] * xs_h
    ys = ys.reshape(bsz, l, D_SSM) * jax.nn.silu(z.astype(jnp.float32))
    ys = grouped_rms_norm(ys, ssm_norm, SSM_GROUPS).astype(x.dtype)
    x = x + jnp.concatenate([ya, ys], axis=-1) @ w_out
    x = x + FFN_RES_WEIGHT * swiglu_ffn(rms_norm(x, ffn2_norm), ffn2_w_gate, ffn2_w_up, ffn2_w_down)
    return x, new_conv_a, new_conv_b, new_ssm.astype(ssm_state.dtype)


def setup_inputs(seed: int = 0) -> dict:
    key = jax.random.key(seed)
    ks = jax.random.split(key, 32)
    f32 = jnp.float32
    L = DEPTH

    def nrm(k, shape, scale):
        return scale * jax.random.normal(k, shape, f32)

    def gain(k, shape):
        return 1.0 + 0.02 * jax.random.normal(k, shape, f32)

    dt0 = jnp.exp(jax.random.uniform(ks[0], (L, SSM_HEADS), f32, math.log(1e-3), math.log(1e-1)))
    dt_bias = dt0 + jnp.log(-jnp.expm1(-dt0))
    return {
        "x_prompt": nrm(ks[1], (BATCH, SEQ, D_MODEL), 1.0),
        "x_sample": nrm(ks[2], (DEC_BATCH, DEC_SEQ, D_MODEL), 1.0),
        "state_conv_a": nrm(ks[3], (L, DEC_BATCH, CONV_A_WIDTH - 1, D_CONV), 0.5),
        "state_conv_b": nrm(ks[4], (L, DEC_BATCH, SSM_CONV_WIDTH - 1, D_XBC), 0.5),
        "state_ssm": nrm(ks[5], (L, DEC_BATCH, SSM_HEADS, SSM_HEAD_DIM, D_STATE), 0.1),
        "ffn1_norm": gain(ks[6], (L, D_MODEL)),
        "ffn1_w_gate": nrm(ks[7], (L, D_MODEL, D_FF), D_MODEL ** -0.5),
        "ffn1_w_up": nrm(ks[8], (L, D_MODEL, D_FF), D_MODEL ** -0.5),
        "ffn1_w_down": nrm(ks[9], (L, D_FF, D_MODEL), D_FF ** -0.5),
        "mix_norm": gain(ks[10], (L, D_MODEL)),
        "w_in": nrm(ks[11], (L, D_MODEL, D_IN), D_MODEL ** -0.5),
        "conv_dw_w": nrm(ks[12], (L, CONV_A_WIDTH, D_CONV), CONV_A_WIDTH ** -0.5),
        "conv_dw_b": nrm(ks[13], (L, D_CONV), 0.02),
        "conv_ln_g": gain(ks[14], (L, D_CONV)),
        "conv_ln_b": nrm(ks[15], (L, D_CONV), 0.02),
        "ssm_conv_w": nrm(ks[16], (L, SSM_CONV_WIDTH, D_XBC), SSM_CONV_WIDTH ** -0.5),
        "ssm_conv_b": nrm(ks[17], (L, D_XBC), 0.02),
        "ssm_dt_bias": dt_bias,
        "ssm_a_log": jnp.log(jax.random.uniform(ks[18], (L, SSM_HEADS), f32, 1.0, 16.0)),
        "ssm_d": 1.0 + 0.1 * jax.random.normal(ks[19], (L, SSM_HEADS), f32),
        "ssm_norm": gain(ks[20], (L, D_SSM)),
        "w_out": nrm(ks[21], (L, D_MIX, D_MODEL), D_MIX ** -0.5),
        "ffn2_norm": gain(ks[22], (L, D_MODEL)),
        "ffn2_w_gate": nrm(ks[23], (L, D_MODEL, D_FF), D_MODEL ** -0.5),
        "ffn2_w_up": nrm(ks[24], (L, D_MODEL, D_FF), D_MODEL ** -0.5),
        "ffn2_w_down": nrm(ks[25], (L, D_FF, D_MODEL), D_FF ** -0.5),
        "final_norm": gain(ks[26], (D_MODEL,)),
    }


def reference(x_prompt, x_sample, state_conv_a, state_conv_b, state_ssm,
              ffn1_norm, ffn1_w_gate, ffn1_w_up, ffn1_w_down, mix_norm, w_in,
              conv_dw_w, conv_dw_b, conv_ln_g, conv_ln_b,
              ssm_conv_w, ssm_conv_b, ssm_dt_bias, ssm_a_log, ssm_d, ssm_norm,
              w_out, ffn2_norm, ffn2_w_gate, ffn2_w_up, ffn2_w_down, final_norm):
    layer_ws = (ffn1_norm, ffn1_w_gate, ffn1_w_up, ffn1_w_down, mix_norm, w_in,
                conv_dw_w, conv_dw_b, conv_ln_g, conv_ln_b,
                ssm_conv_w, ssm_conv_b, ssm_dt_bias, ssm_a_log, ssm_d, ssm_norm,
                w_out, ffn2_norm, ffn2_w_gate, ffn2_w_up, ffn2_w_down)
    bp = x_prompt.shape[0]
    yp, ysm = x_prompt, x_sample
    pa, pb, ps, sa, sb, ss = [], [], [], [], [], []
    for i in range(DEPTH):
        p = tuple(w[i] for w in layer_ws)
        zeros_a = jnp.zeros((bp, CONV_A_WIDTH - 1, D_CONV), yp.dtype)
        zeros_b = jnp.zeros((bp, SSM_CONV_WIDTH - 1, D_XBC), yp.dtype)
        zeros_s = jnp.zeros((bp, SSM_HEADS, SSM_HEAD_DIM, D_STATE), state_ssm.dtype)
        yp, na, nb, ns = decoder_layer(yp, zeros_a, zeros_b, zeros_s, p)
        pa.append(na); pb.append(nb); ps.append(ns)
        ysm, na, nb, ns = decoder_layer(ysm, state_conv_a[i], state_conv_b[i], state_ssm[i], p)
        sa.append(na); sb.append(nb); ss.append(ns)
    y_prompt = rms_norm(yp, final_norm)
    y_sample = rms_norm(ysm, final_norm)
    return (y_prompt, y_sample,
            jnp.stack(pa), jnp.stack(pb), jnp.stack(ps),
            jnp.stack(sa), jnp.stack(sb), jnp.stack(ss))
```

```python
import numpy as np
from contextlib import ExitStack
import concourse.bass as bass
import concourse.mybir as mybir
from concourse.bass_utils import run_bass_kernel_spmd

F32 = mybir.dt.float32
BF16 = mybir.dt.bfloat16
AF = mybir.ActivationFunctionType
ALU = mybir.AluOpType
AX = mybir.AxisListType

EPS = 1e-5
D = 1024
DFF = 2816
NFT = 22
DIN = 4624
RING_K = 6
SLOT = 4096

C_ID = 0
C_TRI = 128
C_EXP = 256
C_INDA = 1280
C_INDB = 1344
C_NEG = 1360
NCONST = 1488
P_WA = 0
P_BA = 248
P_LNG = 256
P_LNB = 264
P_WB = 272
P_BB = 320
P_DTB = 332
P_ALOG = 333
NPV = 334
R_FFN1, R_MIX, R_FFN2, R_FINAL, R_SSMN = 0, 1024, 2048, 3072, 4096
R_D = 5120
NRV = 5136


class Tok:
    __slots__ = ("sem", "val")

    def __init__(self, sem, val=None):
        self.sem = sem
        self.val = val


class Res:
    def __init__(self, name, excl=False, parent=None):
        self.name = name
        self.w = None
        self.rs = {}
        self.excl = excl
        self.parent = parent
        self.kids = {}

    def sub(self, k):
        if k not in self.kids:
            self.kids[k] = Res(f"{self.name}.{k}", self.excl, self)
        return self.kids[k]

    def related(self):
        if self.parent is not None:
            return (self, self.parent)
        return (self,) + tuple(self.kids.values())


class Eng:
    def __init__(self, name, eng, sem, selfsync=True):
        self.name, self.e, self.sem, self.selfsync = name, eng, sem, selfsync
        self.cnt = 0
        self.seen = {}
        self.cur = Tok(sem)
        self.nwaits = 0
        self.nops = 0
        self.pending = False

    def wait(self, tok):
        if tok is None:
            return
        if tok.sem is self.sem and not self.selfsync:
            return
        if tok.val is None:
            if tok.sem is self.sem:
                return
            raise RuntimeError(f"{self.name}: dependency on an unsignalled op")
        k = id(tok.sem)
        if self.seen.get(k, 0) >= tok.val:
            return
        self.e.wait_ge(tok.sem, tok.val)
        self.seen[k] = tok.val
        self.nwaits += 1

    def deps(self, reads, writes):
        for r0 in reads:
            for r in r0.related():
                self.wait(r.w)
                if r.excl:
                    for t in list(r.rs.values()):
                        if t.sem is not self.sem:
                            self.wait(t)
        for w0 in writes:
            for w in w0.related():
                self.wait(w.w)
                for t in list(w.rs.values()):
                    self.wait(t)

    def record(self, tok, reads, writes):
        for r in reads:
            r.rs[id(tok.sem)] = tok
        for w in writes:
            w.w = tok
            w.rs = {}
            for k in w.kids.values():
                k.w = None
                k.rs = {}

    def op(self, fn, reads=(), writes=(), sig=True):
        self.deps(reads, writes)
        ins = fn(self.e)
        tok = self.cur
        self.record(tok, reads, writes)
        self.nops += 1
        self.pending = not sig
        if sig:
            self.cnt += 1
            ins.then_inc(self.sem, 1)
            tok.val = self.cnt
            self.cur = Tok(self.sem)
        return ins


class DmaQ:
    def __init__(self, eng, sems):
        self.eng = eng
        self.sems = sems
        self.vals = [0] * len(sems)
        self.last = [None] * len(sems)
        self.k = 0
        self.all_toks = []

    def dma(self, out, in_, reads=(), writes=(), **kw):
        i = self.k % len(self.sems)
        self.k += 1
        self.eng.wait(self.last[i])
        self.eng.deps(reads, writes)
        ins = self.eng.e.dma_start(out=out, in_=in_, **kw)
        self.vals[i] += 16
        ins.then_inc(self.sems[i], 16)
        tok = Tok(self.sems[i], self.vals[i])
        self.last[i] = tok
        self.eng.record(tok, reads, writes)
        return tok


class Prog:
    def __init__(self, taps=()):
        self.taps = set(taps)
        self.tap_shapes = {}
        nc = self.nc = bass.Bass("TRN2", target_bir_lowering=False)
        self.es = ExitStack()
        self.sem_i = 0

        def sem(name):
            return self.es.enter_context(nc.semaphore(name))

        self.PE = Eng("pe", nc.tensor, sem("s_pe"), selfsync=False)
        self.ACT = Eng("act", nc.scalar, sem("s_act"))
        self.DVE = Eng("dve", nc.vector, sem("s_dve"))
        self.POOL = Eng("pool", nc.gpsimd, sem("s_pool"))
        self.SP = Eng("sp", nc.sync, sem("s_sp"))
        self.engs = [self.PE, self.ACT, self.DVE, self.POOL, self.SP]
        self.QS = DmaQ(self.SP, [sem(f"dq_s{i}") for i in range(12)])
        self.QW = DmaQ(self.POOL, [sem(f"dq_w{i}") for i in range(RING_K + 2)])
        self.QA = DmaQ(self.ACT, [sem(f"dq_a{i}") for i in range(4)])
        self.out_toks = []

    def din(self, name, shape, dt=F32):
        return self.nc.dram_tensor(name, list(shape), dt, kind="ExternalInput").ap()

    def dout(self, name, shape):
        return self.nc.dram_tensor(name, list(shape), F32, kind="ExternalOutput").ap()

    def sb(self, stack, name, shape, dt=F32):
        self.sem_i += 1
        return stack.enter_context(self.nc.sbuf_tensor(f"{name}_{self.sem_i}", list(shape), dt))

    def barrier(self, dma=True):
        for q in ((self.QS, self.QA) if dma else ()):
            for t in q.last:
                if t is not None:
                    for e in self.engs:
                        e.wait(t)
        assert not self.PE.pending, "PE has unsignalled ops at a barrier"
        for e in self.engs:
            for o in self.engs:
                if o.cnt == 0 or (o is e and not e.selfsync):
                    continue
                e.wait(Tok(o.sem, o.cnt))

    def tap(self, name, ap, shape, reads):
        if name not in self.taps:
            return
        d = self.dout("dbg_" + name, shape)
        self.tap_shapes[name] = shape
        t = self.QS.dma(d, ap, reads=reads)
        self.out_toks.append(t)


class _Stop(Exception):
    pass


def build_program(taps=(), stop_after=None, stop_at=None):
    P = Prog(taps)

    def ckpt(name):
        if stop_at is not None and name == stop_at:
            raise _Stop()

    nc = P.nc
    PE, ACT, DVE, POOL, SP, QS, QW = P.PE, P.ACT, P.DVE, P.POOL, P.SP, P.QS, P.QW
    es = P.es

    xp = P.din("xp", [2048, D])
    xsm = P.din("xsm", [16, D])
    sca = P.din("sca", [16, 30, D])
    scb = P.din("scb", [16, 3, 1536])
    sss = P.din("sss", [16, 1024, 128])
    wgate = [P.din("w1g", [D, DFF]), P.din("w2g", [D, DFF])]
    wup = [P.din("w1u", [D, DFF]), P.din("w2u", [D, DFF])]
    wdown = [P.din("w1d", [DFF, D]), P.din("w2d", [DFF, D])]
    w_in = P.din("w_in", [D, DIN])
    w_out = P.din("w_out", [2048, D])
    consts = P.din("consts", [128, NCONST])
    pvec = P.din("pvec", [128, NPV])
    rvec = P.din("rvec", [NRV])
    warep = P.din("warep", [120, D])
    wbrep = P.din("wbrep", [48, 1536])

    yp = P.dout("yp", [2048, D])
    ysm = P.dout("ysm", [16, D])
    nca_p = P.dout("nca_p", [30, D])
    ncb_p = P.dout("ncb_p", [3, 1536])
    nss_p = P.dout("nss_p", [1024, 128])
    nca_s = P.dout("nca_s", [16, 30, D])
    ncb_s = P.dout("ncb_s", [16, 3, 1536])
    nss_s = P.dout("nss_s", [16, 1024, 128])

    cst = P.sb(es, "cst", [128, NCONST]); r_cst = Res("cst")
    idb = P.sb(es, "idb", [128, 128], BF16)
    onb = P.sb(es, "onb", [128, 128], BF16)
    negb = P.sb(es, "negb", [128, 128], BF16)
    pv = P.sb(es, "pv", [128, NPV]); r_pv = Res("pv")
    aneg = P.sb(es, "aneg", [16, 1])
    grep1 = P.sb(es, "grep1", [128, D]); r_grep1 = Res("grep1")
    ssmn = P.sb(es, "ssmn", [128, D])
    drep = P.sb(es, "drep", [128, 16])
    x_sb = P.sb(es, "x_sb", [128, 9, D]); r_x = [Res(f"x{j}") for j in range(9)]
    hT = P.sb(es, "hT", [128, 8, 1040], BF16); r_hT = Res("hT")
    ring = P.sb(es, "ring", [128, RING_K, SLOT], BF16); r_slot = [Res(f"slot{i}") for i in range(RING_K)]
    wdt = P.sb(es, "wdt", [128, 8, 16], BF16); r_wdt = Res("wdt")
    hst = P.sb(es, "hst", [128, D]); r_hst = Res("hst")
    hstb = P.sb(es, "hstb", [128, D], BF16); r_hstb = Res("hstb")
    uhalo = P.sb(es, "uhalo", [128, 8, 30], BF16); r_uhalo = Res("uhalo")
    bhalo = P.sb(es, "bhalo", [128, 12, 3]); r_bhalo = Res("bhalo")
    csAT = P.sb(es, "csAT", [128, 8, 16]); r_csAT = Res("csAT")
    csBT = P.sb(es, "csBT", [128, 12, 16]); r_csBT = Res("csBT")
    usamp = P.sb(es, "usamp", [128, 8, 16]); r_usamp = Res("usamp")
    xrs = P.sb(es, "xrs", [128, 12, 16]); r_xrs = Res("xrs")
    stat = P.sb(es, "stat", [128, 64]); r_stat = Res("stat")
    hn = P.sb(es, "hn", [128, 2, D], BF16); r_hn = [Res("hn0"), Res("hn1")]
    junk = hn[:, 1, :]; r_junk = r_hn[1]

    ps = [es.enter_context(nc.psum_tensor(f"ps{i}", [128, 512], F32)) for i in range(7)]
    r_ps = [Res(f"ps{i}", excl=True) for i in range(7)]
    psb = es.enter_context(nc.psum_tensor("psb", [128, 1024], BF16)); r_psb = Res("psb", excl=True)

    ident = cst[:, C_ID:C_ID + 128]
    tri = cst[:, C_TRI:C_TRI + 128]

    def pvc(col, rows=128):
        return pv[0:rows, col:col + 1]

    def item_src(spec):
        kind = spec[0]
        if kind in ("wg", "wu"):
            _, which, q = spec
            W = (wgate if kind == "wg" else wup)[which]
            ncol = 512 if q < 5 else 256
            return W.rearrange("(k p) c -> p k c", p=128)[:, :, 512 * q:512 * q + ncol], 8, ncol
        if kind == "wd":
            _, which, i = spec
            nf = 4 if i < 5 else 2
            return wdown[which].rearrange("(f p) c -> p f c", p=128)[:, 4 * i:4 * i + nf, :], nf, 1024
        if kind in ("wa", "wb", "wz", "wx"):
            base = {"wa": 0, "wb": 1024, "wz": 2048, "wx": 3072}[kind]
            q = spec[1]
            return w_in.rearrange("(k p) c -> p k c", p=128)[:, :, base + 512 * q:base + 512 * q + 512], 8, 512
        if kind == "wo":
            i = spec[1]
            return w_out.rearrange("(f p) c -> p f c", p=128)[:, 4 * i:4 * i + 4, :], 4, 1024
        raise ValueError(spec)

    def ffn_items(which):
        s = []
        for q in range(6):
            s += [("wg", which, q), ("wu", which, q)]
        s += [("wd", which, i) for i in range(6)]
        return s

    def block_items(b):
        s = ffn_items(0)
        s += [("wa", 0), ("wb", 0), ("wa", 1), ("wb", 1), ("wo", 0), ("wo", 1)]
        nsub = 2 + (1 if b == 1 else 0)
        for _ in range(nsub):
            s += [("wx", 0), ("wx", 1), ("wx", 2), ("wz", 0), ("wz", 1), ("wo", 2), ("wo", 3)]
        s += ffn_items(1)
        return s

    seq = block_items(0) + block_items(1)
    ring_state = {"loaded": 0, "next": 0, "released": set()}

    def ring_try_load():
        while ring_state["loaded"] < len(seq):
            m = ring_state["loaded"]
            if m >= RING_K and (m - RING_K) not in ring_state["released"]:
                break
            src, kt, ncol = item_src(seq[m])
            s = m % RING_K
            dst = ring[:, s, 0:kt * ncol].rearrange("p (k c) -> p k c", k=kt)
            QW.dma(dst, src, writes=[r_slot[s]])
            ring_state["loaded"] += 1

    def ring_get(spec):
        m = ring_state["next"]
        assert seq[m] == spec, (m, seq[m], spec)
        ring_state["next"] += 1
        ring_try_load()
        assert ring_state["loaded"] > m
        _, kt, ncol = item_src(spec)
        s = m % RING_K
        return m, ring[:, s, 0:kt * ncol].rearrange("p (k c) -> p k c", k=kt), r_slot[s]

    def ring_release(m):
        ring_state["released"].add(m)
        ring_try_load()

    def mm(out, lhsT, rhs, start, stop, reads, writes, sig):
        PE.op(lambda e: e.matmul(out, lhsT=lhsT, rhs=rhs, start=start, stop=stop), reads=reads, writes=writes, sig=sig)

    def tr(out, in_, idn, reads, writes, sig=True):
        PE.op(lambda e: e.transpose(out, in_, idn), reads=reads, writes=writes, sig=sig)

    def act(out, in_, func, reads, writes, bias=None, scale=None, accum_out=None):
        kw = {}
        if bias is not None:
            kw["bias"] = bias
        if scale is not None:
            kw["scale"] = scale
        if accum_out is not None:
            kw["accum_out"] = accum_out
        ACT.op(lambda e: e.activation(out=out, in_=in_, func=func, **kw), reads=reads, writes=writes)

    def tt(E, out, in0, in1, op, reads, writes):
        E.op(lambda e: e.tensor_tensor(out=out, in0=in0, in1=in1, op=op), reads=reads, writes=writes)

    def ts(E, out, in0, s1, op0, reads, writes, s2=None, op1=None):
        if op1 is None:
            E.op(lambda e: e.tensor_scalar(out=out, in0=in0, scalar1=s1, scalar2=None, op0=op0), reads=reads, writes=writes)
        else:
            E.op(lambda e: e.tensor_scalar(out=out, in0=in0, scalar1=s1, scalar2=s2, op0=op0, op1=op1), reads=reads, writes=writes)

    def stt(out, in0, scalar, in1, op0, op1, reads, writes):
        DVE.op(lambda e: e.scalar_tensor_tensor(out=out, in0=in0, scalar=scalar, in1=in1, op0=op0, op1=op1), reads=reads, writes=writes)

    def cp(E, out, in_, reads, writes):
        if E is ACT:
            E.op(lambda e: e.copy(out=out, in_=in_), reads=reads, writes=writes)
        else:
            E.op(lambda e: e.tensor_copy(out=out, in_=in_), reads=reads, writes=writes)

    def memset(E, ap, val, writes):
        E.op(lambda e: e.memset(ap, val), writes=writes)

    def block_geom(b):
        tiles = [(j, 128, 128 * j) for j in range(8)]
        chunks = [(0, 512), (512, 512)]
        if b == 1:
            tiles.append((8, 16, 1024))
            chunks.append((1024, 16))
        return tiles, chunks

    QS.dma(cst[:], consts[:], writes=[r_cst])
    QS.dma(pv[:], pvec[:], writes=[r_pv])
    QS.dma(grep1[:], rvec[R_FFN1:R_FFN1 + D].partition_broadcast(128), writes=[r_grep1])
    r_ssmn = Res("ssmn"); r_drep = Res("drep")
    QS.dma(ssmn[:], rvec[R_SSMN:R_SSMN + D].partition_broadcast(128), writes=[r_ssmn])
    QS.dma(drep[:], rvec[R_D:R_D + 16].partition_broadcast(128), writes=[r_drep])
    QW.dma(wdt[:], w_in.rearrange("(k p) c -> p k c", p=128)[:, :, 4608:4624], writes=[r_wdt])
    r_idb = Res("idb"); r_onb = Res("onb"); r_aneg = Res("aneg")
    cp(DVE, idb[:], ident, [r_cst], [r_idb])
    r_negb = Res("negb")
    cp(DVE, negb[:], cst[:, C_NEG:C_NEG + 128], [r_cst], [r_negb])
    memset(DVE, onb[:], 1.0, [r_onb])
    memset(DVE, stat[:], 1.0, [r_stat])
    memset(DVE, stat[:, 48:50], -0.5, [r_stat])
    act(aneg[:], pvc(P_ALOG, 16), AF.Exp, [r_pv], [r_aneg])
    ts(DVE, aneg[:], aneg[:], -1.0, ALU.mult, [r_aneg], [r_aneg])
    memset(POOL, hst[:], 0.0, [r_hst])
    memset(POOL, hstb[:], 0.0, [r_hstb])
    ring_try_load()

    with ExitStack() as ph:
        bufA = P.sb(ph, "bufA", [120, 4, D]); r_bufA = Res("bufA")
        wra = P.sb(ph, "wra", [120, D]); r_wra = Res("wra")
        cs_tm = P.sb(ph, "cs_tm", [16, 1536]); r_cs = Res("cs_tm")
        bufB = P.sb(ph, "bufB", [48, 1536]); r_bufB = Res("bufB")
        wrb = P.sb(ph, "wrb", [48, 1536]); r_wrb = Res("wrb")
        QS.dma(bufA[:], sca.rearrange("(t bl) k c -> (bl k) t c", bl=4), writes=[r_bufA])
        QS.dma(wra[:], warep[:], writes=[r_wra])
        QS.dma(bufB[:], scb.rearrange("b k c -> (b k) c"), writes=[r_bufB])
        QS.dma(wrb[:], wbrep[:], writes=[r_wrb])
        for jj in range(8):
            QS.dma(x_sb[:, jj, :], xp[128 * jj:128 * jj + 128, :], writes=[r_x[jj]])
        P.out_toks.append(QS.dma(nca_s[:, 0:29, :], sca[:, 1:30, :]))
        P.out_toks.append(QS.dma(ncb_s[:, 0:2, :], scb[:, 1:3, :]))
        tt(POOL, bufA[:, 0:2, :], bufA[:, 0:2, :], wra[:].unsqueeze(1).to_broadcast([120, 2, D]), ALU.mult, [r_bufA.sub(0), r_wra], [r_bufA.sub(0)])
        tt(DVE, bufA[:, 2:4, :], bufA[:, 2:4, :], wra[:].unsqueeze(1).to_broadcast([120, 2, D]), ALU.mult, [r_bufA.sub(1), r_wra], [r_bufA.sub(1)])
        tt(POOL, bufB[:], bufB[:], wrb[:], ALU.mult, [r_bufB, r_wrb], [r_bufB])
        for dc in range(2):
            for t in range(4):
                mm(ps[dc][0:16, :], cst[0:120, C_INDA + 16 * t:C_INDA + 16 * t + 16], bufA[:, t, dc * 512:(dc + 1) * 512],
                   t == 0, t == 3, [r_cst, r_bufA], [r_ps[dc]], t == 3)
            cp(ACT, cs_tm[:, dc * 512:(dc + 1) * 512], ps[dc][0:16, :], [r_ps[dc]], [r_cs])
        for c in range(8):
            tr(ps[2][:, 16 * c:16 * c + 16], cs_tm[0:16, 128 * c:128 * c + 128], cst[0:16, C_ID:C_ID + 16], [r_cs, r_cst], [r_ps[2]], c == 7)
        cp(ACT, csAT[:].rearrange("p a b -> p (a b)"), ps[2][:, 0:128], [r_ps[2]], [r_csAT])
        for dc in range(3):
            mm(ps[3 + dc][0:16, :], cst[0:48, C_INDB:C_INDB + 16], bufB[:, dc * 512:(dc + 1) * 512], True, True,
               [r_cst, r_bufB], [r_ps[3 + dc]], True)
            cp(ACT, cs_tm[:, dc * 512:(dc + 1) * 512], ps[3 + dc][0:16, :], [r_ps[3 + dc], r_cs], [r_cs])
        for c in range(12):
            tr(ps[2][:, 16 * c:16 * c + 16], cs_tm[0:16, 128 * c:128 * c + 128], cst[0:16, C_ID:C_ID + 16], [r_cs, r_cst], [r_ps[2]], c == 11)
        cp(ACT, csBT[:].rearrange("p a b -> p (a b)"), ps[2][:, 0:192], [r_ps[2]], [r_csBT])
    P.barrier()

    def load_gain(off):
        QS.dma(grep1[:], rvec[off:off + D].partition_broadcast(128), writes=[r_grep1])

    def tile_rstd(tiles, col0):
        for (j, rows, c0) in tiles:
            act(junk[:rows, :], x_sb[:rows, j, :], AF.Square, [r_x[j]], [r_junk, r_stat],
                accum_out=stat[:rows, col0 + j:col0 + j + 1])
        n = len(tiles)
        act(stat[:, col0:col0 + n], stat[:, col0:col0 + n], AF.Sqrt, [r_stat], [r_stat], bias=EPS, scale=1.0 / D)
        DVE.op(lambda e: e.reciprocal(out=stat[:, col0:col0 + n], in_=stat[:, col0:col0 + n]), reads=[r_stat], writes=[r_stat])

    def emit_norm(b, gain_off):
        tiles, _ = block_geom(b)

        def sq(j, rows, c0):
            act(hstb[:rows, :], x_sb[:rows, j, :], AF.Square, [r_x[j]], [r_hstb, r_stat.sub(j)],
                accum_out=stat[:rows, j:j + 1])

        def rest(j, rows, c0):
            par = j % 2
            ts(POOL, stat[:rows, 16 + j:17 + j], stat[:rows, j:j + 1], 1.0 / D, ALU.mult, [r_stat.sub(j)], [r_stat.sub(16 + j)], s2=EPS, op1=ALU.add)
            tt(POOL, stat[:rows, 16 + j:17 + j], stat[:rows, 16 + j:17 + j], stat[:rows, 48:49], ALU.pow, [r_stat.sub(16 + j), r_stat.sub(48)],
               [r_stat.sub(16 + j)])
            stt(hn[:rows, par, :], x_sb[:rows, j, :], stat[:rows, 16 + j:17 + j], grep1[:rows, :], ALU.mult, ALU.mult,
                [r_x[j], r_stat.sub(16 + j), r_grep1], [r_hn[par]])
            tgt, r_tgt = (psb[:], r_psb) if par == 0 else (ps[5][:].bitcast(BF16), r_ps[5])
            for k in range(8):
                tr(tgt[:, 128 * k:128 * k + rows], hn[:rows, par, 128 * k:128 * k + 128], idb[:rows, :rows],
                   [r_hn[par], r_idb], [r_tgt], k == 7)
            cp(ACT, hT[:, :, c0:c0 + rows], tgt.rearrange("p (k t) -> p k t", k=8)[:, :, 0:rows], [r_tgt], [r_hT.sub(j)])

        LAG = 3
        for i, tl in enumerate(tiles):
            sq(*tl)
            if i >= LAG:
                rest(*tiles[i - LAG])
        for tl in tiles[max(0, len(tiles) - LAG):]:
            rest(*tl)

    def emit_ffn(b, which):
        tiles, chunks = block_geom(b)
        with ExitStack() as ph:
            hid = P.sb(ph, "hid", [128, NFT, 1040], BF16); r_hid = [Res(f"hid{f}") for f in range(NFT)]
            sg = P.sb(ph, "sg", [128, 2, 1040]); r_sg = [Res("sg0"), Res("sg1")]
            if which == 1:
                fin_g = P.sb(ph, "fin_g", [128, D]); r_fing = Res("fin_g")
                QS.dma(fin_g[:], rvec[R_FINAL:R_FINAL + D].partition_broadcast(128), writes=[r_fing])
            for q in range(6):
                mg, sG, rG = ring_get(("wg", which, q))
                mu, sU, rU = ring_get(("wu", which, q))
                for fl in range(4 if q < 5 else 2):
                    f = 4 * q + fl
                    par = f % 2
                    tb = 4 + par
                    for (slot, rS, b0, toff) in ((sG, rG, 0, 0), (sU, rU, 2, 16)):
                        for k in range(8):
                            for ci, (c0, n) in enumerate(chunks):
                                if ci < 2:
                                    o = ps[b0 + ci][:, 0:n]; ro = r_ps[b0 + ci]
                                else:
                                    o = ps[tb][:, toff:toff + n]; ro = r_ps[tb]
                                mm(o, slot[:, k, 128 * fl:128 * fl + 128], hT[:, k, c0:c0 + n], k == 0, k == 7,
                                   [rS, r_hT], [ro], k == 7 and ci == len(chunks) - 1)
                    for ci, (c0, n) in enumerate(chunks):
                        gi = ps[ci][:, 0:n] if ci < 2 else ps[tb][:, 0:n]
                        rgi = r_ps[ci] if ci < 2 else r_ps[tb]
                        act(sg[:, par, c0:c0 + n], gi, AF.Silu, [rgi], [r_sg[par].sub(ci)])
                    for ci, (c0, n) in enumerate(chunks):
                        ui = ps[2 + ci][:, 0:n] if ci < 2 else ps[tb][:, 16:16 + n]
                        rui = r_ps[2 + ci] if ci < 2 else r_ps[tb]
                        tt(DVE, hid[:, f, c0:c0 + n], ui, sg[:, par, c0:c0 + n], ALU.mult, [rui, r_sg[par].sub(ci)], [r_hid[f].sub(ci)])
                ring_release(mg)
                ring_release(mu)
            for grp in ((0, 1, 2), (3, 4, 5)):
                got = [ring_get(("wd", which, i)) for i in grp]
                fl_list = []
                for gi, i in enumerate(grp):
                    for fl in range(4 if i < 5 else 2):
                        fl_list.append((4 * i + fl, got[gi][1], got[gi][2], fl))
                for (j, rows, c0) in tiles:
                    for dc in range(2):
                        bk = 2 * (j % 2) + dc
                        for idx, (f, slot, rS, fl) in enumerate(fl_list):
                            mm(ps[bk][:rows, :], hid[:, f, c0:c0 + rows], slot[:, fl, dc * 512:(dc + 1) * 512],
                               idx == 0, idx == len(fl_list) - 1, [r_hid[f], rS], [r_ps[bk]], idx == len(fl_list) - 1)
                        xs_ = x_sb[:rows, j, dc * 512:(dc + 1) * 512]
                        stt(xs_, ps[bk][:rows, :], 0.5, xs_, ALU.mult, ALU.add, [r_ps[bk], r_x[j]], [r_x[j]])
                    if which == 1 and grp[0] == 3:
                        final_tile(b, j, rows, fin_g, r_fing)
                for g in got:
                    ring_release(g[0])
        P.barrier(dma=False)

    def emit_groupA(b):
        tiles, chunks = block_geom(b)
        T = 1040 if b == 1 else 1024
        with ExitStack() as ph:
            ua = P.sb(ph, "ua", [128, 8, 1040]); r_ua = [[Res(f"ua{c}_{h}") for h in range(3)] for c in range(8)]
            yaT = P.sb(ph, "yaT", [128, 8, 1040], BF16); r_yaT = Res("yaT")
            with ExitStack() as ph2:
                uext = P.sb(ph2, "uext", [128, 2, 1054], BF16); r_uext = [Res("uext0"), Res("uext1")]
                sig = P.sb(ph2, "sig", [128, 1040]); r_sig = Res("sig")
                dg = P.sb(ph2, "dg", [128, 2, 31, 128], BF16); r_dg = [Res("dg0"), Res("dg1")]
                utail = P.sb(ph2, "utail", [128, 30]); r_utail = Res("utail")
                tailA = grep1; r_tailA = r_grep1
                ustm = sig; r_ustm = r_sig
                ga_slots = {}

                def ga_glu(c):
                    q, cl = c // 4, c % 4
                    par = c % 2
                    if cl == 0:
                        ga_slots[q] = (ring_get(("wa", q)), ring_get(("wb", q)))
                    (ma, sA, rA), (mb, sB, rB) = ga_slots[q]
                    tt(POOL, dg[:, par, :, :], ident.unsqueeze(1).to_broadcast([128, 31, 128]),
                       pv[:, P_WA + 31 * c:P_WA + 31 * c + 31].unsqueeze(2).to_broadcast([128, 31, 128]), ALU.mult,
                       [r_cst, r_pv], [r_dg[par]])
                    for (slot, rS, b0, toff) in ((sB, rB, 2, 16), (sA, rA, 0, 0)):
                        for k in range(8):
                            for ci, (c0, n) in enumerate(chunks):
                                if ci < 2:
                                    o = ps[b0 + ci][:, 0:n]; ro = r_ps[b0 + ci]
                                else:
                                    o = ps[4][:, toff:toff + n]; ro = r_ps[4]
                                mm(o, slot[:, k, 128 * cl:128 * cl + 128], hT[:, k, c0:c0 + n], k == 0, k == 7,
                                   [rS, r_hT], [ro], k == 7 and ci == len(chunks) - 1)
                        if b0 == 2:
                            for ci, (c0, n) in enumerate(chunks[0:2]):
                                act(sig[:, c0:c0 + n], ps[2 + ci][:, 0:n], AF.Sigmoid, [r_ps[2 + ci]], [r_sig.sub(ci)])
                    if cl == 3:
                        ring_release(ma)
                        ring_release(mb)
                    if len(chunks) > 2:
                        c0, n = chunks[2]
                        act(sig[:, c0:c0 + n], ps[4][:, 16:16 + n], AF.Sigmoid, [r_ps[4]], [r_sig.sub(2)])
                    if b == 0:
                        memset(POOL, uext[:, par, 0:30], 0.0, [r_uext[par]])
                    else:
                        cp(POOL, uext[:, par, 0:30], uhalo[:, c, :], [r_uhalo], [r_uext[par]])
                    for ci, (c0, n) in enumerate(chunks):
                        if ci < 2:
                            tt(DVE, uext[:, par, 30 + c0:30 + c0 + n], ps[ci][:, 0:n], sig[:, c0:c0 + n], ALU.mult,
                               [r_ps[ci], r_sig.sub(ci)], [r_uext[par]])
                        else:
                            tt(DVE, usamp[:, c, :], ps[4][:, 0:16], sig[:, c0:c0 + n], ALU.mult, [r_ps[4], r_sig.sub(2)], [r_usamp])
                    if b == 0:
                        cp(POOL, uhalo[:, c, :], uext[:, par, 1024:1054], [r_uext[par]], [r_uhalo])
                    else:
                        tt(DVE, utail[:, :], ps[1][:, 482:512], sig[:, 994:1024], ALU.mult, [r_ps[1], r_sig.sub(1)], [r_utail])
                        tr(ps[4][0:30, 128:256], utail[:, :], ident, [r_utail, r_cst], [r_ps[4]], True)
                        cp(ACT, tailA[0:30, 128 * c:128 * c + 128], ps[4][0:30, 128:256], [r_ps[4]], [r_tailA])

                def ga_conv(c):
                    par = c % 2
                    for hh in range(2):
                        bkc = 5 + hh
                        for k in range(31):
                            mm(ps[bkc][:, :], dg[:, par, k, :], uext[:, par, 512 * hh + k:512 * hh + k + 512], k == 0, k == 30,
                               [r_dg[par], r_uext[par]], [r_ps[bkc]], k == 30)
                        act(ua[:, c, 512 * hh:512 * hh + 512], ps[bkc][:, :], AF.Identity, [r_ps[bkc], r_pv], [r_ua[c][hh]],
                            bias=pvc(P_BA + c))
                    if b == 1:
                        o = ua[:, c, 1024:1040]
                        ts(DVE, o, csAT[:, c, :], pvc(P_BA + c), ALU.add, [r_csAT, r_pv], [r_ua[c][2]])
                        stt(o, usamp[:, c, :], pvc(P_WA + 31 * c + 30), o, ALU.mult, ALU.add, [r_usamp, r_pv, r_ua[c][2]], [r_ua[c][2]])

                ga_glu(0)
                for c in range(8):
                    if c + 1 < 8:
                        ga_glu(c + 1)
                    ga_conv(c)
                if b == 1:
                    P.out_toks.append(QS.dma(nca_p[:, :], tailA[0:30, :], reads=[r_tailA]))
                    for c in range(8):
                        bk = 5 + c // 4
                        tr(ps[bk][0:16, 128 * (c % 4):128 * (c % 4) + 128], usamp[:, c, :], ident, [r_usamp, r_cst], [r_ps[bk]], True)
                        if c % 4 == 3:
                            cp(ACT, ustm[0:16, 512 * (c // 4):512 * (c // 4) + 512], ps[bk][0:16, :], [r_ps[bk]], [r_ustm])
                    P.out_toks.append(QS.dma(nca_s[:, 29, :], ustm[0:16, 0:D], reads=[r_ustm]))
                P.tap(f"ua{b}", ua[:, :, 0:1024], [128, 8, 1024], [x for c in range(8) for x in r_ua[c]])
            P.barrier()
            with ExitStack() as ph2:
                LNW = 512
                uab = P.sb(ph2, "uab", [128, 8, LNW], BF16); r_uab = Res("uab")
                sqb = P.sb(ph2, "sqb", [128, 8, LNW], BF16); r_sqb = Res("sqb")
                mst = P.sb(ph2, "mst", [128, 3, LNW]); r_mst = Res("mst")
                r_uaall = Res("ua_all")
                lnchunks = [(c0 + o, min(LNW, n - o)) for (c0, n) in chunks for o in range(0, n, LNW)]
                def ln_a(i):
                    c0, n = lnchunks[i]
                    bs = 2 * (i % 2)
                    for c in range(8):
                        cp(POOL, uab[:, c, 0:n], ua[:, c, c0:c0 + n], [r_uaall.sub(c)], [r_uab.sub(c)])
                        if c % 2 == 0:
                            act(sqb[:, c, 0:n], ua[:, c, c0:c0 + n], AF.Square, [r_uaall.sub(c)], [r_sqb.sub(c)])
                        else:
                            tt(POOL, sqb[:, c, 0:n], ua[:, c, c0:c0 + n], ua[:, c, c0:c0 + n], ALU.mult, [r_uaall.sub(c)], [r_sqb.sub(c)])
                    for c in range(8):
                        mm(ps[bs][:, 0:n], onb[:], uab[:, c, 0:n], c == 0, c == 7, [r_onb, r_uab], [r_ps[bs]], c == 7)
                    for c in range(8):
                        mm(ps[bs + 1][:, 0:n], onb[:], sqb[:, c, 0:n], c == 0, c == 7, [r_onb, r_sqb], [r_ps[bs + 1]], c == 7)

                def ln_b(i):
                    c0, n = lnchunks[i]
                    bs = 2 * (i % 2)
                    ts(DVE, mst[:, 0, 0:n], ps[bs][:, 0:n], 1.0 / D, ALU.mult, [r_ps[bs]], [r_mst])
                    tt(DVE, mst[:, 1, 0:n], mst[:, 0, 0:n], mst[:, 0, 0:n], ALU.mult, [r_mst], [r_mst])
                    stt(mst[:, 2, 0:n], ps[bs + 1][:, 0:n], 1.0 / D, mst[:, 1, 0:n], ALU.mult, ALU.subtract, [r_ps[bs + 1], r_mst], [r_mst])
                    act(mst[:, 2, 0:n], mst[:, 2, 0:n], AF.Sqrt, [r_mst], [r_mst], bias=EPS, scale=1.0)
                    DVE.op(lambda e: e.reciprocal(out=mst[:, 2, 0:n], in_=mst[:, 2, 0:n]), reads=[r_mst], writes=[r_mst])
                    for c in range(8):
                        o = ua[:, c, c0:c0 + n]
                        tt(DVE, o, o, mst[:, 0, 0:n], ALU.subtract, [r_uaall.sub(c), r_mst], [r_uaall.sub(c)])
                        tt(DVE, o, o, mst[:, 2, 0:n], ALU.mult, [r_uaall.sub(c), r_mst], [r_uaall.sub(c)])
                        act(yaT[:, c, c0:c0 + n], o, AF.Silu, [r_uaall.sub(c), r_pv], [r_yaT.sub(c)], bias=pvc(P_LNB + c), scale=pvc(P_LNG + c))

                ln_a(0)
                for i in range(len(lnchunks)):
                    if i + 1 < len(lnchunks):
                        ln_a(i + 1)
                    ln_b(i)
            got = [ring_get(("wo", 0)), ring_get(("wo", 1))]
            for (j, rows, c0) in tiles:
                for dc in range(2):
                    bk = 2 * (j % 2) + dc
                    for k in range(8):
                        mm(ps[bk][:rows, :], yaT[:, k, c0:c0 + rows], got[k // 4][1][:, k % 4, dc * 512:(dc + 1) * 512],
                           k == 0, k == 7, [r_yaT, got[k // 4][2]], [r_ps[bk]], k == 7)
                    xs_ = x_sb[:rows, j, dc * 512:(dc + 1) * 512]
                    tt(DVE, xs_, ps[bk][:rows, :], xs_, ALU.add, [r_ps[bk], r_x[j]], [r_x[j]])
            for g in got:
                ring_release(g[0])
        P.barrier()

    def emit_groupB(b):
        subs = [(0, 512, False), (512, 512, False)]
        if b == 1:
            subs.append((1024, 16, True))
        for (s0, S, is_s) in subs:
            with ExitStack() as ph:
                nt = 4 if not is_s else 1
                rows = 128 if not is_s else 16
                xs_tm = P.sb(ph, "xs_tm", [128, nt, D]); r_xstm = [Res(f"xstm{t}") for t in range(4)]
                SS = 512 if not is_s else 16
                BT = P.sb(ph, "BT", [128, 2, SS], BF16); r_BT = Res("BT")
                CT = P.sb(ph, "CT", [128, 2, SS], BF16); r_CT = Res("CT")
                Btm = P.sb(ph, "Btm", [128, nt, 256], BF16); r_Btm = Res("Btm")
                dta = P.sb(ph, "dta", [128, nt, 32]); r_dta = Res("dta")
                dtT = P.sb(ph, "dtT", [16, 4, SS] if is_s else [16, 3, SS]); r_dtT = Res("dtT")
                xraw = P.sb(ph, "xraw", [128, 2, SS + 3]); r_xraw = [Res("xraw0"), Res("xraw1")]
                xc = P.sb(ph, "xc", [128, 2, SS]); r_xc = [Res("xc0"), Res("xc1")]
                xsT = xc; r_xsT = r_xc
                y2 = P.sb(ph, "y_sb", [128, 2, D]); r_y2 = [Res("y0"), Res("y1")]
                y_sb = y2[:, 0, :]; r_y = r_y2[0]
                zs2 = P.sb(ph, "zs_sb", [128, 2, D]); r_zs2 = [Res("zs0"), Res("zs1")]
                zs_sb = zs2[:, 0, :]; r_zs = r_zs2[0]
                gn = hn[:, 0, :]; r_gn = Res("gn")
                ysT = P.sb(ph, "ysT", [128, 8, 128], BF16); r_ysT = Res("ysT")
                xs_sT = P.sb(ph, "xs_sT", [128, 8, 16]); r_xssT = Res("xs_sT")
                BCs = P.sb(ph, "BCs", [128, 4, 16]); r_BCs = Res("BCs")

                b1_slots = {}

                def b1_mm(f):
                    i, fl = f // 4, f % 4
                    if fl == 0:
                        b1_slots[i] = ring_get(("wx", i))
                    mx, sX, rX = b1_slots[i]
                    bk = f % 2
                    for k in range(8):
                        mm(ps[bk][:, 0:S], sX[:, k, 128 * fl:128 * fl + 128], hT[:, k, s0:s0 + S], k == 0, k == 7,
                           [rX, r_hT], [r_ps[bk]], k == 7)
                    if fl == 3:
                        ring_release(mx)

                def b1_a(f):
                    par = f % 2
                    bk = f % 2
                    wk = lambda kk: pvc(P_WB + 4 * f + kk)
                    if not is_s:
                        cp(ACT, xraw[:, par, 3:3 + S], ps[bk][:, 0:S], [r_ps[bk]], [r_xraw[par]])
                        if b == 0 and s0 == 0:
                            memset(POOL, xraw[:, par, 0:3], 0.0, [r_xraw[par]])
                        else:
                            cp(POOL, xraw[:, par, 0:3], bhalo[:, f, :], [r_bhalo], [r_xraw[par]])
                        cp(POOL, bhalo[:, f, :], xraw[:, par, S:S + 3], [r_xraw[par]], [r_bhalo])
                        o = xc[:, par, 0:S]
                        ts(DVE, o, xraw[:, par, 0:S], wk(0), ALU.mult, [r_xraw[par], r_pv], [r_xc[par]], s2=pvc(P_BB + f), op1=ALU.add)
                        for kk in range(1, 4):
                            stt(o, xraw[:, par, kk:kk + S], wk(kk), o, ALU.mult, ALU.add, [r_xraw[par], r_pv, r_xc[par]], [r_xc[par]])
                    else:
                        cp(ACT, xrs[:, f, :], ps[bk][:, 0:16], [r_ps[bk]], [r_xrs])
                        o = xc[:, par, 0:16]
                        ts(DVE, o, csBT[:, f, :], pvc(P_BB + f), ALU.add, [r_csBT, r_pv], [r_xc[par]])
                        stt(o, xrs[:, f, :], wk(3), o, ALU.mult, ALU.add, [r_xrs, r_pv, r_xc[par]], [r_xc[par]])

                def b1_b(f):
                    par = f % 2
                    if not is_s:
                        o = xc[:, par, 0:S]
                        if f < 8:
                            act(xsT[:, par, 0:S], o, AF.Silu, [r_xc[par]], [r_xsT[par]])
                            bk2 = 2 + f % 2
                            for t in range(4):
                                tr(ps[bk2][:, 128 * t:128 * t + 128], xsT[:, par, 128 * t:128 * t + 128], ident,
                                   [r_xsT[par], r_cst], [r_ps[bk2]], t == 3)
                            cp(ACT, xs_tm[:, :, 128 * f:128 * f + 128],
                               ps[bk2][:].rearrange("p (t c) -> p t c", t=4), [r_ps[bk2]], r_xstm)
                        elif f < 10:
                            g = f - 8
                            act(BT[:, g, 0:S], o, AF.Silu, [r_xc[par]], [r_BT])
                            for t in range(4):
                                tr(psb[:, 128 * t:128 * t + 128], BT[:, g, 128 * t:128 * t + 128], idb[:], [r_BT, r_idb], [r_psb], t == 3)
                            cp(ACT, Btm[:, :, 128 * g:128 * g + 128], psb[:, 0:512].rearrange("p (t c) -> p t c", t=4), [r_psb], [r_Btm])
                        else:
                            g = f - 10
                            act(CT[:, g, 0:S], o, AF.Silu, [r_xc[par]], [r_CT])
                    else:
                        o = xc[:, par, 0:16]
                        if f < 8:
                            act(xs_sT[:, f, :], o, AF.Silu, [r_xc[par]], [r_xssT])
                        else:
                            act(BCs[:, f - 8, :], o, AF.Silu, [r_xc[par]], [r_BCs])

                for k in range(8):
                    mm(ps[4][0:16, 0:S], wdt[:, k, :], hT[:, k, s0:s0 + S], k == 0, k == 7, [r_wdt, r_hT], [r_ps[4]], k == 7)
                xb_, tm_, dt_ = (dtT[:, i, 0:S] for i in range(3))
                a_ = dtT[:, 3, 0:S] if is_s else xb_
                A_IDX = 3 if is_s else 0
                act(xb_, ps[4][0:16, 0:S], AF.Identity, [r_ps[4], r_pv], [r_dtT], bias=pvc(P_DTB, 16))
                act(tm_, xb_, AF.Abs, [r_dtT], [r_dtT])
                act(tm_, tm_, AF.Exp, [r_dtT], [r_dtT], scale=-1.0)
                act(tm_, tm_, AF.Ln, [r_dtT], [r_dtT], bias=1.0)
                ts(DVE, dt_, xb_, 0.0, ALU.max, [r_dtT], [r_dtT])
                tt(DVE, dt_, dt_, tm_, ALU.add, [r_dtT], [r_dtT])
                ts(DVE, a_, dt_, aneg[:, 0:1], ALU.mult, [r_dtT, r_aneg], [r_dtT])
                b1_mm(0)
                b1_a(0)
                for f in range(12):
                    if f + 1 < 12:
                        b1_mm(f + 1)
                        b1_a(f + 1)
                    b1_b(f)
                if not is_s:
                    for t in range(4):
                        tr(ps[5][:, 32 * t:32 * t + 16], dtT[:, 2, 128 * t:128 * t + 128], cst[0:16, C_ID:C_ID + 16], [r_dtT, r_cst], [r_ps[5]], False)
                        tr(ps[5][:, 32 * t + 16:32 * t + 32], dtT[:, A_IDX, 128 * t:128 * t + 128], cst[0:16, C_ID:C_ID + 16], [r_dtT, r_cst], [r_ps[5]], t == 3)
                    cp(ACT, dta[:].rearrange("p a b -> p (a b)"), ps[5][:, 0:128], [r_ps[5]], [r_dta])
                if b == 1 and s0 == 512:
                    for f in range(12):
                        bk = 5 + (f // 4) % 2
                        tr(ps[bk][0:3, 128 * (f % 4):128 * (f % 4) + 128], bhalo[:, f, :], ident, [r_bhalo, r_cst], [r_ps[bk]], True)
                        if f % 4 == 3:
                            dstb, rdst = (zs_sb[0:3, 512 * (f // 4):512 * (f // 4) + 512], r_zs) if f < 8 else (y_sb[0:3, 0:512], r_y)
                            cp(ACT, dstb, ps[bk][0:3, :], [r_ps[bk]], [rdst])
                    P.out_toks.append(QS.dma(ncb_p[:, 0:1024], zs_sb[0:3, :], reads=[r_zs]))
                    P.out_toks.append(QS.dma(ncb_p[:, 1024:1536], y_sb[0:3, 0:512], reads=[r_y]))
                if is_s:
                    for f in range(12):
                        bk = 5 + (f // 4) % 2
                        tr(ps[bk][0:16, 128 * (f % 4):128 * (f % 4) + 128], xrs[:, f, :], ident, [r_xrs, r_cst], [r_ps[bk]], True)
                        if f % 4 == 3:
                            dstb, rdst = (zs_sb[0:16, 512 * (f // 4):512 * (f // 4) + 512], r_zs) if f < 8 else (y_sb[0:16, 0:512], r_y)
                            cp(ACT, dstb, ps[bk][0:16, :], [r_ps[bk]], [rdst])
                    P.out_toks.append(QS.dma(ncb_s[:, 2, 0:1024], zs_sb[0:16, :], reads=[r_zs]))
                    P.out_toks.append(QS.dma(ncb_s[:, 2, 1024:1536], y_sb[0:16, 0:512], reads=[r_y]))

                ckpt(f"b1_{b}_{s0}")
                gz = [ring_get(("wz", 0)), ring_get(("wz", 1))]
                go = [ring_get(("wo", 2)), ring_get(("wo", 3))]

                def zproj(rows, c0, zp):
                    for dc in range(2):
                        for k in range(8):
                            mm(ps[2 + dc][:rows, :], hT[:, k, c0:c0 + rows], gz[dc][1][:, k, :], k == 0, k == 7,
                               [r_hT, gz[dc][2]], [r_ps[2 + dc]], k == 7)
                        act(zs2[:rows, zp, 512 * dc:512 * dc + 512], ps[2 + dc][:rows, :], AF.Silu, [r_ps[2 + dc]], [r_zs2[zp].sub(dc)])

                def post1(rows, r_xs_t, xs_t, zp, yp):
                    zs_sb = zs2[:, zp, :]
                    r_zs = r_zs2[zp]
                    y_sb = y2[:, yp, :]
                    r_y = r_y2[yp]
                    tt(DVE, y_sb[:rows, :], y_sb[:rows, :], xs_t, ALU.add, [r_y, r_xs_t], [r_y])
                    tt(DVE, y_sb[:rows, :], y_sb[:rows, :], zs_sb[:rows, :], ALU.mult, [r_y, r_zs], [r_y])
                    for g in range(2):
                        act(junk[:rows, 512 * g:512 * g + 512], y_sb[:rows, 512 * g:512 * g + 512], AF.Square, [r_y], [r_junk.sub(g), r_stat.sub(32 + g)],
                            accum_out=stat[:rows, 32 + g:33 + g])
                    ts(POOL, stat[:rows, 34:36], stat[:rows, 32:34], 1.0 / 512, ALU.mult, [r_stat.sub(32), r_stat.sub(33)], [r_stat.sub(34)],
                       s2=EPS, op1=ALU.add)
                    tt(POOL, stat[:rows, 36:38], stat[:rows, 34:36], stat[:rows, 48:50], ALU.pow, [r_stat.sub(34), r_stat.sub(48)], [r_stat.sub(36)])
                    for g in range(2):
                        stt(gn[:rows, 512 * g:512 * g + 512], y_sb[:rows, 512 * g:512 * g + 512], stat[:rows, 36 + g:37 + g],
                            ssmn[:rows, 512 * g:512 * g + 512], ALU.mult, ALU.mult, [r_y, r_stat.sub(36), r_ssmn], [r_gn.sub(g)])

                def post2(rows):
                    for k in range(8):
                        tr(psb[:, 128 * k:128 * k + rows], gn[:rows, 128 * k:128 * k + 128], idb[:rows, :rows], [r_gn, r_idb], [r_psb], k == 7)
                    cp(ACT, ysT[:, :, 0:rows], psb[:].rearrange("p (k t) -> p k t", k=8)[:, :, 0:rows], [r_psb], [r_ysT])

                def post3(rows, j):
                    for dc in range(2):
                        for k in range(8):
                            mm(ps[2 + dc][:rows, :], ysT[:, k, 0:rows], go[k // 4][1][:, k % 4, 512 * dc:512 * dc + 512], k == 0, k == 7,
                               [r_ysT, go[k // 4][2]], [r_ps[2 + dc]], k == 7)
                        xs_ = x_sb[:rows, j, dc * 512:(dc + 1) * 512]
                        tt(DVE, xs_, ps[2 + dc][:rows, :], xs_, ALU.add, [r_ps[2 + dc], r_x[j]], [r_x[j]])

                if not is_s:
                    E = P.sb(ph, "E", [128, 1, 8, 128]); r_E = [Res("E0")] * 2
                    CBm = P.sb(ph, "CBm", [128, 2, 128]); r_CBm = Res("CBm")
                    MT = P.sb(ph, "MT", [128, 2, 16, 128], BF16); r_MT = [Res("MT0"), Res("MT1")]
                    Xb = P.sb(ph, "Xb", [128, 2, D], BF16); r_Xb = [Res("Xb0"), Res("Xb1")]
                    Xd = P.sb(ph, "Xd", [128, 2, D], BF16); r_Xd = [Res("Xd0"), Res("Xd1")]
                    sm = P.sb(ph, "sm", [128, 2, 8, 16]); r_sm = [Res("sm0"), Res("sm1")]

                    def smv(t):
                        pp = t % 2
                        return [sm[:, pp, i, :] for i in range(8)], r_sm[pp], pp

                    def f_pe(t):
                        a_tm = dta[:, t, 16:32]
                        cols = slice(128 * t, 128 * t + 128)
                        mm(ps[6][:, 0:16], tri, a_tm, True, True, [r_cst, r_dta], [r_ps[6]], False)
                        for g in range(2):
                            mm(ps[6][:, 128 + 128 * g:256 + 128 * g], BT[:, g, cols], CT[:, g, cols], True, True, [r_BT, r_CT], [r_ps[6]], g == 1)
                        f_arow(t, 0)
                        zproj(128, s0 + 128 * t, t % 2)

                    def f_arow(t, g):
                        for h8 in range(8):
                            h = 8 * g + h8
                            o = ps[h8 // 4][:, 128 * (h8 % 4):128 * (h8 % 4) + 128]
                            mm(o, dta[:, t, 16 + h:17 + h].to_broadcast([128, 128]), tri, True, False, [r_dta, r_cst], [r_ps[h8 // 4]], False)
                            mm(o, idb[:], negb[:], False, True, [r_idb, r_negb], [r_ps[h8 // 4]], h8 % 4 == 3)

                    def f_a(t):
                        (acs, nacs, e2, cdr, dte, dtd, tmp16, alast), rsm, pp = smv(t)
                        cp(ACT, acs, ps[6][:, 0:16], [r_ps[6]], [rsm.sub("acs")])
                        ACT.op(lambda e: e.mul(out=nacs, in_=acs, mul=-1.0), reads=[rsm.sub("acs")], writes=[rsm.sub("nacs")])
                        cp(ACT, CBm[:].rearrange("p g i -> p (g i)"), ps[6][:, 128:384], [r_ps[6]], [r_CBm])

                    def f_g(t, g):
                        (acs, nacs, e2, cdr, dte, dtd, tmp16, alast), rsm, pp = smv(t)
                        for q2 in range(2):
                            lastc = ps[q2][:].rearrange("p (h i) -> p h i", h=4)[:, :, 127]
                            q4 = 2 * g + q2
                            cp(ACT, alast[:, 4 * q4:4 * q4 + 4], lastc, [r_ps[q2]], [rsm.sub(f"alast{q4}")])
                        for h8 in range(8):
                            h = 8 * g + h8
                            act(E[:, 0, h8, :], ps[h8 // 4][:, 128 * (h8 % 4):128 * (h8 % 4) + 128], AF.Exp, [r_ps[h8 // 4], rsm.sub("nacs")], [r_E[0].sub(h8)],
                                bias=sm[:, pp, 1, h:h + 1])
                        if g == 0:
                            f_arow(t, 1)
                        tt(DVE, MT[:, pp, 8 * g:8 * g + 8, :], E[:, 0, :, :], CBm[:, g:g + 1, :].to_broadcast([128, 8, 128]), ALU.mult,
                           [r_E[0], r_CBm], [r_MT[pp].sub(g)])

                    def f_tail(t):
                        (acs, nacs, e2, cdr, dte, dtd, tmp16, alast), rsm, pp = smv(t)
                        dt_tm = dta[:, t, 0:16]
                        r_al = [rsm.sub(f"alast{q4}") for q4 in range(4)]
                        act(e2, acs, AF.Exp, [rsm.sub("acs")], [rsm.sub("e2")])
                        act(cdr, alast, AF.Exp, r_al, [rsm.sub("cdr")])
                        tt(DVE, tmp16, alast, acs, ALU.subtract, r_al + [rsm.sub("acs")], [rsm.sub("tmp")])
                        act(dte, tmp16, AF.Exp, [rsm.sub("tmp")], [rsm.sub("dte")])
                        tt(DVE, dtd, dt_tm, dte, ALU.mult, [r_dta, rsm.sub("dte")], [rsm.sub("dtd")])
                        xs3 = xs_tm[:, t, :].rearrange("p (h d) -> p h d", h=16)
                        tt(POOL, Xb[:, pp, :].rearrange("p (h d) -> p h d", h=16), xs3, dt_tm.unsqueeze(2).to_broadcast([128, 16, 64]), ALU.mult,
                           [r_xstm[t], r_dta], [r_Xb[pp]])
                        tt(POOL, Xd[:, pp, :].rearrange("p (h d) -> p h d", h=16), xs3, dtd.unsqueeze(2).to_broadcast([128, 16, 64]), ALU.mult,
                           [r_xstm[t], rsm.sub("dtd")], [r_Xd[pp]])
                        tt(POOL, xs3, xs3, drep[:].unsqueeze(2).to_broadcast([128, 16, 64]), ALU.mult, [r_xstm[t], r_drep], [r_xstm[t]])

                    def head1(t):
                        (acs, nacs, e2, cdr, dte, dtd, tmp16, alast), rsm, pp = smv(t)
                        cols = slice(128 * t, 128 * t + 128)
                        for g in range(2):
                            mm(ps[4 + g][:, :], CT[:, g, cols], hstb[:, 512 * g:512 * g + 512], True, True, [r_CT, r_hstb], [r_ps[4 + g]], True)
                        for g in range(2):
                            yv = y2[:, pp, 512 * g:512 * g + 512].rearrange("p (h d) -> p h d", h=8)
                            tt(DVE, yv, ps[4 + g][:].rearrange("p (h d) -> p h d", h=8), e2[:, 8 * g:8 * g + 8].unsqueeze(2).to_broadcast([128, 8, 64]),
                               ALU.mult, [r_ps[4 + g], rsm.sub("e2")], [r_y2[pp].sub(g)])

                    def head2(t):
                        (acs, nacs, e2, cdr, dte, dtd, tmp16, alast), rsm, pp = smv(t)
                        for h in range(16):
                            mm(ps[4 + h // 8][:, 64 * (h % 8):64 * (h % 8) + 64], MT[:, pp, h, :], Xb[:, pp, 64 * h:64 * h + 64], True, True,
                               [r_MT[pp], r_Xb[pp]], [r_ps[4 + h // 8]], h % 8 == 7)
                        for g in range(2):
                            tt(DVE, y2[:, pp, 512 * g:512 * g + 512], y2[:, pp, 512 * g:512 * g + 512], ps[4 + g][:, :], ALU.add,
                               [r_y2[pp].sub(g), r_ps[4 + g]], [r_y2[pp].sub(g)])

                    def head3(t):
                        (acs, nacs, e2, cdr, dte, dtd, tmp16, alast), rsm, pp = smv(t)
                        h3 = hst[:].rearrange("p (h d) -> p h d", h=16)
                        tt(DVE, h3, h3, cdr.unsqueeze(2).to_broadcast([128, 16, 64]), ALU.mult, [r_hst, rsm.sub("cdr")], [r_hst])
                        for g in range(2):
                            mm(ps[4 + g][:, :], Btm[:, t, 128 * g:128 * g + 128], Xd[:, pp, 512 * g:512 * g + 512], True, True,
                               [r_Btm, r_Xd[pp]], [r_ps[4 + g]], True)
                        for g in range(2):
                            tt(DVE, hst[:, 512 * g:512 * g + 512], hst[:, 512 * g:512 * g + 512], ps[4 + g][:, :], ALU.add, [r_hst, r_ps[4 + g]], [r_hst])
                        cp(ACT, hstb[:], hst[:], [r_hst], [r_hstb])

                    cp(ACT, hstb[:], hst[:], [r_hst], [r_hstb])
                    f_pe(0)
                    f_a(0)
                    f_g(0, 0)
                    f_g(0, 1)
                    f_tail(0)
                    for t in range(5):
                        hd = t < 4
                        fr = t + 1 < 4
                        po = t >= 1
                        if po:
                            post1(128, r_xstm[t - 1], xs_tm[:, t - 1, :], (t - 1) % 2, (t - 1) % 2)
                        if hd:
                            head1(t)
                        if fr:
                            f_pe(t + 1)
                        if hd:
                            head2(t)
                        if po:
                            post2(128)
                        if fr:
                            f_a(t + 1)
                            f_g(t + 1, 0)
                        if hd:
                            head3(t)
                        if po:
                            post3(128, s0 // 128 + t - 1)
                        if fr:
                            f_g(t + 1, 1)
                            f_tail(t + 1)
                    ckpt(f"b2_{b}_{s0}")
                    if b == 1 and s0 == 512:
                        ho = E[:, 0, :, :]; r_ho = r_E[0]
                        for jj in range(8):
                            bk = 5 + (jj // 4) % 2
                            tr(ps[bk][:, 128 * (jj % 4):128 * (jj % 4) + 128], hst[:, 128 * jj:128 * jj + 128], ident, [r_hst, r_cst], [r_ps[bk]], True)
                            if jj % 4 == 3:
                                cp(ACT, ho[:, 4 * (jj // 4):4 * (jj // 4) + 4, :], ps[bk][:].rearrange("p (a n) -> p a n", a=4), [r_ps[bk]], [r_ho])
                        P.out_toks.append(QS.dma(nss_p.rearrange("(j m) n -> m j n", m=128), ho, reads=[r_ho]))
                else:
                    dec = P.sb(ph, "dec", [128, 8, 16]); r_dec = Res("dec")
                    dtx = P.sb(ph, "dtx", [128, 8, 16]); r_dtx = Res("dtx")
                    ysT_s = P.sb(ph, "ysT_s", [128, 8, 16]); r_ysTs = Res("ysT_s")
                    h0 = P.sb(ph, "h0", [128, 2, 8, 128]); r_h0 = [Res("h0a"), Res("h0b")]
                    h1 = P.sb(ph, "h1", [128, 2, 8, 128]); r_h1 = [Res("h1a"), Res("h1b")]
                    t2 = P.sb(ph, "t2", [128, 2, 8, 128]); r_t2 = [Res("t2a"), Res("t2b")]
                    t3 = P.sb(ph, "t3", [128, 2, 8, 128]); r_t3 = [Res("t3a"), Res("t3b")]
                    ckpt("s_pre")
                    for jj in range(8):
                        ex = cst[0:16, C_EXP + 128 * jj:C_EXP + 128 * jj + 128]
                        mm(ps[6][:, 16 * jj:16 * jj + 16], ex, dtT[:, 2, 0:16], True, True, [r_cst, r_dtT], [r_ps[6]], False)
                        mm(ps[6][:, 128 + 16 * jj:128 + 16 * jj + 16], ex, dtT[:, 3, 0:16], True, True, [r_cst, r_dtT], [r_ps[6]], jj == 7)
                    ckpt("s_mm")
                    act(dec[:].rearrange("p a b -> p (a b)"), ps[6][:, 128:256], AF.Exp, [r_ps[6]], [r_dec])
                    ckpt("s_act")
                    tt(DVE, dtx[:].rearrange("p a b -> p (a b)"), ps[6][:, 0:128], xs_sT[:].rearrange("p a b -> p (a b)"), ALU.mult,
                       [r_ps[6], r_xssT], [r_dtx])
                    ckpt("s_exp")
                    for bb in range(16):
                        par = bb % 2
                        bk = bb % 4
                        ckpt(f"s_b{bb}")
                        if bb == 0:
                            for b2 in range(2):
                                QS.dma(h0[:, b2, :, :], sss[b2].rearrange("(j m) n -> m j n", m=128), writes=[r_h0[b2]])
                        for g in range(2):
                            mm(ps[bk][:, 128 * g:128 * g + 128], BCs[:, g, bb:bb + 1].to_broadcast([128, 128]), ident, True, True,
                               [r_BCs, r_cst], [r_ps[bk]], False)
                            mm(ps[bk][:, 256 + 128 * g:384 + 128 * g], BCs[:, 2 + g, bb:bb + 1].to_broadcast([128, 128]), ident, True, True,
                               [r_BCs, r_cst], [r_ps[bk]], g == 1)
                        tt(POOL, h1[:, par, :, :], h0[:, par, :, :], dec[:, :, bb:bb + 1].to_broadcast([128, 8, 128]), ALU.mult,
                           [r_h0[par], r_dec], [r_h1[par]])
                        if bb + 2 < 16:
                            QS.dma(h0[:, par, :, :], sss[bb + 2].rearrange("(j m) n -> m j n", m=128), writes=[r_h0[par]])
                        for g in range(2):
                            tt(DVE, t2[:, par, 4 * g:4 * g + 4, :], ps[bk][:, 128 * g:128 * g + 128].unsqueeze(1).to_broadcast([128, 4, 128]),
                               dtx[:, 4 * g:4 * g + 4, bb:bb + 1].to_broadcast([128, 4, 128]), ALU.mult, [r_ps[bk], r_dtx], [r_t2[par].sub(g)])
                        tt(POOL, h1[:, par, :, :], h1[:, par, :, :], t2[:, par, :, :], ALU.add, [r_h1[par], r_t2[par]], [r_h1[par]])
                        P.out_toks.append(QS.dma(nss_s[bb].rearrange("(j m) n -> m j n", m=128), h1[:, par, :, :], reads=[r_h1[par]]))
                        for g in range(2):
                            tt(DVE, t3[:, par, 4 * g:4 * g + 4, :], h1[:, par, 4 * g:4 * g + 4, :],
                               ps[bk][:, 256 + 128 * g:384 + 128 * g].unsqueeze(1).to_broadcast([128, 4, 128]), ALU.mult,
                               [r_h1[par], r_ps[bk]], [r_t3[par].sub(g)])
                        DVE.op(lambda e: e.tensor_reduce(out=ysT_s[:, :, bb], in_=t3[:, par, :, :], axis=AX.X, op=ALU.add), reads=[r_t3[par]], writes=[r_ysTs.sub(bb)])
                    ckpt("s_loop")
                    for jj in range(8):
                        bk = 5 + (jj // 4) % 2
                        tr(ps[bk][0:16, 128 * (jj % 4):128 * (jj % 4) + 128], ysT_s[:, jj, :], ident, [r_ysTs, r_cst], [r_ps[bk]], True)
                        if jj % 4 == 3:
                            cp(ACT, y_sb[0:16, 512 * (jj // 4):512 * (jj // 4) + 512], ps[bk][0:16, :], [r_ps[bk]], [r_y])
                    for jj in range(8):
                        bk = 5 + (jj // 4) % 2
                        tr(ps[bk][0:16, 128 * (jj % 4):128 * (jj % 4) + 128], xs_sT[:, jj, :], ident, [r_xssT, r_cst], [r_ps[bk]], True)
                        if jj % 4 == 3:
                            cp(ACT, xs_tm[0:16, 0, 512 * (jj // 4):512 * (jj // 4) + 512], ps[bk][0:16, :], [r_ps[bk]], [r_xstm[0]])
                    xs3 = xs_tm[0:16, 0, :].rearrange("p (h d) -> p h d", h=16)
                    tt(POOL, xs3, xs3, drep[0:16, :].unsqueeze(2).to_broadcast([16, 16, 64]), ALU.mult, [r_xstm[0], r_drep], [r_xstm[0]])
                    ckpt("s_post")
                    zproj(16, 1024, 0)
                    post1(16, r_xstm[0], xs_tm[0:16, 0, :], 0, 0)
                    post2(16)
                    post3(16, 8)
                for g in gz + go:
                    ring_release(g[0])
            P.barrier()

    def final_tile(b, j, rows, gain, r_gain):
        act(hstb[:rows, :], x_sb[:rows, j, :], AF.Square, [r_x[j]], [r_hstb, r_stat.sub(j)], accum_out=stat[:rows, j:j + 1])
        ts(POOL, stat[:rows, 16 + j:17 + j], stat[:rows, j:j + 1], 1.0 / D, ALU.mult, [r_stat.sub(j)], [r_stat.sub(16 + j)], s2=EPS, op1=ALU.add)
        tt(POOL, stat[:rows, 16 + j:17 + j], stat[:rows, 16 + j:17 + j], stat[:rows, 48:49], ALU.pow, [r_stat.sub(16 + j), r_stat.sub(48)],
           [r_stat.sub(16 + j)])
        stt(x_sb[:rows, j, :], x_sb[:rows, j, :], stat[:rows, 16 + j:17 + j], gain[:rows, :], ALU.mult, ALU.mult,
            [r_x[j], r_stat.sub(16 + j), r_gain], [r_x[j]])
        if j < 8:
            dst = yp[1024 * b + 128 * j:1024 * b + 128 * j + 128, :]
        else:
            dst = ysm[:, :]
        P.out_toks.append(QS.dma(dst, x_sb[:rows, j, :], reads=[r_x[j]]))
        if b == 0:
            P.QA.dma(x_sb[:, j, :], xp[1024 + 128 * j:1024 + 128 * j + 128, :], writes=[r_x[j]])

    def emit_final(b):
        pass

    def load_x(b):
        if b == 0:
            return
        QS.dma(x_sb[0:16, 8, :], xsm[:, :], writes=[r_x[8]])

    def xtap(name, b):
        tiles, _ = block_geom(b)
        P.tap(name, x_sb[:, 0:8, :], [128, 8, D], r_x[0:8])

    stage_no = [0]

    def stage(fn, *a):
        if stop_after is not None and stage_no[0] >= stop_after:
            raise _Stop()
        stage_no[0] += 1
        fn(*a)

    try:
        for b in range(2):
            stage(load_x, b)
            stage(emit_norm, b, R_FFN1)
            load_gain(R_MIX)
            P.tap(f"h1T{b}", hT[:, :, 0:1024], [128, 8, 1024], [r_hT])
            stage(emit_ffn, b, 0)
            xtap(f"x1_{b}", b)
            stage(emit_norm, b, R_MIX)
            stage(emit_groupA, b)
            xtap(f"xa_{b}", b)
            load_gain(R_FFN2)
            stage(emit_groupB, b)
            xtap(f"xb_{b}", b)
            stage(emit_norm, b, R_FFN2)
            if b == 0:
                load_gain(R_FFN1)
            stage(emit_ffn, b, 1)
            xtap(f"x2_{b}", b)
            stage(emit_final, b)
    except _Stop:
        pass

    for t in P.out_toks:
        SP.wait(t)
    assert stop_after is not None or stop_at is not None or ring_state["next"] == len(seq)
    P.stats = {e.name: (e.nops, e.nwaits, e.cnt) for e in P.engs}
    return P


def _consts():
    c = np.zeros((128, NCONST), np.float32)
    c[:, C_ID:C_ID + 128] = np.eye(128, dtype=np.float32)
    c[:, C_TRI:C_TRI + 128] = np.triu(np.ones((128, 128), np.float32))
    for j in range(8):
        for m in range(128):
            c[2 * j + m // 64, C_EXP + 128 * j + m] = 1.0
    for t in range(4):
        for bl in range(4):
            for k in range(30):
                c[bl * 30 + k, C_INDA + 16 * t + 4 * t + bl] = 1.0
    for bq in range(16):
        for k in range(3):
            c[bq * 3 + k, C_INDB + bq] = 1.0
    c[:, C_NEG:C_NEG + 128] = -16384.0 * np.tril(np.ones((128, 128), np.float32), -1)
    return c


def _pvec(inp):
    pv = np.zeros((128, NPV), np.float32)
    wa = inp["conv_dw_w"][0]
    pv[:, P_WA:P_WA + 248] = wa.T.reshape(8, 128, 31).transpose(1, 0, 2).reshape(128, 248)
    for off, key in ((P_BA, "conv_dw_b"), (P_LNG, "conv_ln_g"), (P_LNB, "conv_ln_b")):
        pv[:, off:off + 8] = inp[key][0].reshape(8, 128).T
    wb = inp["ssm_conv_w"][0]
    pv[:, P_WB:P_WB + 48] = wb.T.reshape(12, 128, 4).transpose(1, 0, 2).reshape(128, 48)
    pv[:, P_BB:P_BB + 12] = inp["ssm_conv_b"][0].reshape(12, 128).T
    pv[0:16, P_DTB] = inp["ssm_dt_bias"][0]
    pv[0:16, P_ALOG] = inp["ssm_a_log"][0]
    return pv


_PROG = {}


def _get_prog(taps=(), stop_after=None, stop_at=None):
    key = (tuple(taps), stop_after, stop_at)
    if key not in _PROG:
        _PROG[key] = build_program(taps, stop_after, stop_at)
    return _PROG[key]


def kernel(_taps=(), _stop_after=None, _cores=8, _stop_at=None, **inp):
    inp = {k: np.asarray(v) for k, v in inp.items()}
    P = _get_prog(_taps, _stop_after, _stop_at)
    f32 = lambda a: np.ascontiguousarray(a, dtype=np.float32)
    consts = _consts()
    pvec = _pvec(inp)
    rvec = f32(np.concatenate([inp["ffn1_norm"][0], inp["mix_norm"][0], inp["ffn2_norm"][0], inp["final_norm"],
                               inp["ssm_norm"][0], inp["ssm_d"][0]]))
    warep = f32(np.tile(inp["conv_dw_w"][0][0:30], (4, 1)))
    wbrep = f32(np.tile(inp["ssm_conv_w"][0][0:3], (16, 1)))
    shared = {
        "w1g": f32(inp["ffn1_w_gate"][0]), "w1u": f32(inp["ffn1_w_up"][0]), "w1d": f32(inp["ffn1_w_down"][0]),
        "w2g": f32(inp["ffn2_w_gate"][0]), "w2u": f32(inp["ffn2_w_up"][0]), "w2d": f32(inp["ffn2_w_down"][0]),
        "w_in": f32(inp["w_in"][0]), "w_out": f32(inp["w_out"][0]),
        "consts": consts, "pvec": pvec, "rvec": rvec, "warep": warep, "wbrep": wbrep,
    }
    in_maps = []
    for c in range(_cores):
        sl = slice(16 * c, 16 * c + 16)
        m = dict(shared)
        m["xp"] = f32(inp["x_prompt"][c])
        m["xsm"] = f32(inp["x_sample"][sl, 0])
        m["sca"] = f32(inp["state_conv_a"][0, sl])
        m["scb"] = f32(inp["state_conv_b"][0, sl])
        m["sss"] = f32(inp["state_ssm"][0, sl].reshape(16, 1024, 128))
        in_maps.append(m)
    res = run_bass_kernel_spmd(P.nc, in_maps, core_ids=list(range(_cores)))
    R = res.results
    if _taps:
        kernel.last_taps = [{k: r["dbg_" + k] for k in P.tap_shapes} for r in R]
    n = _cores
    y_prompt = np.stack([R[c]["yp"] for c in range(n)])
    y_sample = np.concatenate([R[c]["ysm"] for c in range(n)])[:, None, :]
    nca_p = np.stack([R[c]["nca_p"] for c in range(n)])[None]
    ncb_p = np.stack([R[c]["ncb_p"] for c in range(n)])[None]
    nss_p = np.stack([R[c]["nss_p"].reshape(16, 64, 128) for c in range(n)])[None]
    nca_s = np.concatenate([R[c]["nca_s"] for c in range(n)])[None]
    ncb_s = np.concatenate([R[c]["ncb_s"] for c in range(n)])[None]
    nss_s = np.concatenate([R[c]["nss_s"].reshape(16, 16, 64, 128) for c in range(n)])[None]
    return tuple(np.ascontiguousarray(a, dtype=np.float32) for a in
                 (y_prompt, y_sample, nca_p, ncb_p, nss_p, nca_s, ncb_s, nss_s))
```

```python
import numpy as np
from contextlib import ExitStack
import concourse.bass as bass
import concourse.mybir as mybir
from concourse.bass_utils import run_bass_kernel_spmd

F32 = mybir.dt.float32
BF16 = mybir.dt.bfloat16
AF = mybir.ActivationFunctionType
ALU = mybir.AluOpType
AX = mybir.AxisListType

EPS = 1e-5
D = 1024
DFF = 2816
NFT = 22
DIN = 4624
RING_K = 6
SLOT = 4096

C_ID = 0
C_TRI = 128
C_EXP = 256
C_INDA = 1280
C_INDB = 1344
C_NEG = 1360
NCONST = 1488
P_WA = 0
P_BA = 248
P_LNG = 256
P_LNB = 264
P_WB = 272
P_BB = 320
P_DTB = 332
P_ALOG = 333
NPV = 334
R_FFN1, R_MIX, R_FFN2, R_FINAL, R_SSMN = 0, 1024, 2048, 3072, 4096
R_D = 5120
NRV = 5136


class Tok:
    __slots__ = ("sem", "val")

    def __init__(self, sem, val=None):
        self.sem = sem
        self.val = val


class Res:
    def __init__(self, name, excl=False, parent=None):
        self.name = name
        self.w = None
        self.rs = {}
        self.excl = excl
        self.parent = parent
        self.kids = {}

    def sub(self, k):
        if k not in self.kids:
            self.kids[k] = Res(f"{self.name}.{k}", self.excl, self)
        return self.kids[k]

    def related(self):
        if self.parent is not None:
            return (self, self.parent)
        return (self,) + tuple(self.kids.values())


class Eng:
    def __init__(self, name, eng, sem, selfsync=True):
        self.name, self.e, self.sem, self.selfsync = name, eng, sem, selfsync
        self.cnt = 0
        self.seen = {}
        self.cur = Tok(sem)
        self.nwaits = 0
        self.nops = 0
        self.pending = False

    def wait(self, tok):
        if tok is None:
            return
        if tok.sem is self.sem and not self.selfsync:
            return
        if tok.val is None:
            if tok.sem is self.sem:
                return
            raise RuntimeError(f"{self.name}: dependency on an unsignalled op")
        k = id(tok.sem)
        if self.seen.get(k, 0) >= tok.val:
            return
        self.e.wait_ge(tok.sem, tok.val)
        self.seen[k] = tok.val
        self.nwaits += 1

    def deps(self, reads, writes):
        for r0 in reads:
            for r in r0.related():
                self.wait(r.w)
                if r.excl:
                    for t in list(r.rs.values()):
                        if t.sem is not self.sem:
                            self.wait(t)
        for w0 in writes:
            for w in w0.related():
                self.wait(w.w)
                for t in list(w.rs.values()):
                    self.wait(t)

    def record(self, tok, reads, writes):
        for r in reads:
            r.rs[id(tok.sem)] = tok
        for w in writes:
            w.w = tok
            w.rs = {}
            for k in w.kids.values():
                k.w = None
                k.rs = {}

    def op(self, fn, reads=(), writes=(), sig=True):
        self.deps(reads, writes)
        ins = fn(self.e)
        tok = self.cur
        self.record(tok, reads, writes)
        self.nops += 1
        self.pending = not sig
        if sig:
            self.cnt += 1
            ins.then_inc(self.sem, 1)
            tok.val = self.cnt
            self.cur = Tok(self.sem)
        return ins


class DmaQ:
    def __init__(self, eng, sems):
        self.eng = eng
        self.sems = sems
        self.vals = [0] * len(sems)
        self.last = [None] * len(sems)
        self.k = 0
        self.all_toks = []

    def dma(self, out, in_, reads=(), writes=(), **kw):
        i = self.k % len(self.sems)
        self.k += 1
        self.eng.wait(self.last[i])
        self.eng.deps(reads, writes)
        ins = self.eng.e.dma_start(out=out, in_=in_, **kw)
        self.vals[i] += 16
        ins.then_inc(self.sems[i], 16)
        tok = Tok(self.sems[i], self.vals[i])
        self.last[i] = tok
        self.eng.record(tok, reads, writes)
        return tok


class Prog:
    def __init__(self, taps=()):
        self.taps = set(taps)
        self.tap_shapes = {}
        nc = self.nc = bass.Bass("TRN2", target_bir_lowering=False)
        self.es = ExitStack()
        self.sem_i = 0

        def sem(name):
            return self.es.enter_context(nc.semaphore(name))

        self.PE = Eng("pe", nc.tensor, sem("s_pe"), selfsync=False)
        self.ACT = Eng("act", nc.scalar, sem("s_act"))
        self.DVE = Eng("dve", nc.vector, sem("s_dve"))
        self.POOL = Eng("pool", nc.gpsimd, sem("s_pool"))
        self.SP = Eng("sp", nc.sync, sem("s_sp"))
        self.engs = [self.PE, self.ACT, self.DVE, self.POOL, self.SP]
        self.QS = DmaQ(self.SP, [sem(f"dq_s{i}") for i in range(12)])
        self.QW = DmaQ(self.POOL, [sem(f"dq_w{i}") for i in range(RING_K + 2)])
        self.out_toks = []

    def din(self, name, shape, dt=F32):
        return self.nc.dram_tensor(name, list(shape), dt, kind="ExternalInput").ap()

    def dout(self, name, shape):
        return self.nc.dram_tensor(name, list(shape), F32, kind="ExternalOutput").ap()

    def sb(self, stack, name, shape, dt=F32):
        self.sem_i += 1
        return stack.enter_context(self.nc.sbuf_tensor(f"{name}_{self.sem_i}", list(shape), dt))

    def barrier(self, dma=True):
        for q in ((self.QS,) if dma else ()):
            for t in q.last:
                if t is not None:
                    for e in self.engs:
                        e.wait(t)
        assert not self.PE.pending, "PE has unsignalled ops at a barrier"
        for e in self.engs:
            for o in self.engs:
                if o.cnt == 0 or (o is e and not e.selfsync):
                    continue
                e.wait(Tok(o.sem, o.cnt))

    def tap(self, name, ap, shape, reads):
        if name not in self.taps:
            return
        d = self.dout("dbg_" + name, shape)
        self.tap_shapes[name] = shape
        t = self.QS.dma(d, ap, reads=reads)
        self.out_toks.append(t)


class _Stop(Exception):
    pass


def build_program(taps=(), stop_after=None, stop_at=None):
    P = Prog(taps)

    def ckpt(name):
        if stop_at is not None and name == stop_at:
            raise _Stop()

    nc = P.nc
    PE, ACT, DVE, POOL, SP, QS, QW = P.PE, P.ACT, P.DVE, P.POOL, P.SP, P.QS, P.QW
    es = P.es

    xp = P.din("xp", [2048, D])
    xsm = P.din("xsm", [16, D])
    sca = P.din("sca", [16, 30, D])
    scb = P.din("scb", [16, 3, 1536])
    sss = P.din("sss", [16, 1024, 128])
    wgate = [P.din("w1g", [D, DFF]), P.din("w2g", [D, DFF])]
    wup = [P.din("w1u", [D, DFF]), P.din("w2u", [D, DFF])]
    wdown = [P.din("w1d", [DFF, D]), P.din("w2d", [DFF, D])]
    w_in = P.din("w_in", [D, DIN])
    w_out = P.din("w_out", [2048, D])
    consts = P.din("consts", [128, NCONST])
    pvec = P.din("pvec", [128, NPV])
    rvec = P.din("rvec", [NRV])
    warep = P.din("warep", [120, D])
    wbrep = P.din("wbrep", [48, 1536])

    yp = P.dout("yp", [2048, D])
    ysm = P.dout("ysm", [16, D])
    nca_p = P.dout("nca_p", [30, D])
    ncb_p = P.dout("ncb_p", [3, 1536])
    nss_p = P.dout("nss_p", [1024, 128])
    nca_s = P.dout("nca_s", [16, 30, D])
    ncb_s = P.dout("ncb_s", [16, 3, 1536])
    nss_s = P.dout("nss_s", [16, 1024, 128])

    cst = P.sb(es, "cst", [128, NCONST]); r_cst = Res("cst")
    idb = P.sb(es, "idb", [128, 128], BF16)
    onb = P.sb(es, "onb", [128, 128], BF16)
    negb = P.sb(es, "negb", [128, 128], BF16)
    pv = P.sb(es, "pv", [128, NPV]); r_pv = Res("pv")
    aneg = P.sb(es, "aneg", [16, 1])
    grep1 = P.sb(es, "grep1", [128, D]); r_grep1 = Res("grep1")
    ssmn = P.sb(es, "ssmn", [128, D])
    drep = P.sb(es, "drep", [128, 16])
    x_sb = P.sb(es, "x_sb", [128, 9, D]); r_x = [Res(f"x{j}") for j in range(9)]
    hT = P.sb(es, "hT", [128, 8, 1040], BF16); r_hT = Res("hT")
    ring = P.sb(es, "ring", [128, RING_K, SLOT], BF16); r_slot = [Res(f"slot{i}") for i in range(RING_K)]
    wdt = P.sb(es, "wdt", [128, 8, 16], BF16); r_wdt = Res("wdt")
    hst = P.sb(es, "hst", [128, D]); r_hst = Res("hst")
    hstb = P.sb(es, "hstb", [128, D], BF16); r_hstb = Res("hstb")
    uhalo = P.sb(es, "uhalo", [128, 8, 30], BF16); r_uhalo = Res("uhalo")
    bhalo = P.sb(es, "bhalo", [128, 12, 3]); r_bhalo = Res("bhalo")
    csAT = P.sb(es, "csAT", [128, 8, 16]); r_csAT = Res("csAT")
    csBT = P.sb(es, "csBT", [128, 12, 16]); r_csBT = Res("csBT")
    usamp = P.sb(es, "usamp", [128, 8, 16]); r_usamp = Res("usamp")
    xrs = P.sb(es, "xrs", [128, 12, 16]); r_xrs = Res("xrs")
    stat = P.sb(es, "stat", [128, 64]); r_stat = Res("stat")
    hn = P.sb(es, "hn", [128, 2, D], BF16); r_hn = [Res("hn0"), Res("hn1")]
    junk = hn[:, 1, :]; r_junk = r_hn[1]

    ps = [es.enter_context(nc.psum_tensor(f"ps{i}", [128, 512], F32)) for i in range(7)]
    r_ps = [Res(f"ps{i}", excl=True) for i in range(7)]
    psb = es.enter_context(nc.psum_tensor("psb", [128, 1024], BF16)); r_psb = Res("psb", excl=True)

    ident = cst[:, C_ID:C_ID + 128]
    tri = cst[:, C_TRI:C_TRI + 128]

    def pvc(col, rows=128):
        return pv[0:rows, col:col + 1]

    def item_src(spec):
        kind = spec[0]
        if kind in ("wg", "wu"):
            _, which, q = spec
            W = (wgate if kind == "wg" else wup)[which]
            ncol = 512 if q < 5 else 256
            return W.rearrange("(k p) c -> p k c", p=128)[:, :, 512 * q:512 * q + ncol], 8, ncol
        if kind == "wd":
            _, which, i = spec
            nf = 4 if i < 5 else 2
            return wdown[which].rearrange("(f p) c -> p f c", p=128)[:, 4 * i:4 * i + nf, :], nf, 1024
        if kind in ("wa", "wb", "wz", "wx"):
            base = {"wa": 0, "wb": 1024, "wz": 2048, "wx": 3072}[kind]
            q = spec[1]
            return w_in.rearrange("(k p) c -> p k c", p=128)[:, :, base + 512 * q:base + 512 * q + 512], 8, 512
        if kind == "wo":
            i = spec[1]
            return w_out.rearrange("(f p) c -> p f c", p=128)[:, 4 * i:4 * i + 4, :], 4, 1024
        raise ValueError(spec)

    def ffn_items(which):
        s = []
        for q in range(6):
            s += [("wg", which, q), ("wu", which, q)]
        s += [("wd", which, i) for i in range(6)]
        return s

    def block_items(b):
        s = ffn_items(0)
        s += [("wa", 0), ("wb", 0), ("wa", 1), ("wb", 1), ("wo", 0), ("wo", 1)]
        nsub = 2 + (1 if b == 1 else 0)
        for _ in range(nsub):
            s += [("wx", 0), ("wx", 1), ("wx", 2), ("wz", 0), ("wz", 1), ("wo", 2), ("wo", 3)]
        s += ffn_items(1)
        return s

    seq = block_items(0) + block_items(1)
    ring_state = {"loaded": 0, "next": 0, "released": set()}

    def ring_try_load():
        while ring_state["loaded"] < len(seq):
            m = ring_state["loaded"]
            if m >= RING_K and (m - RING_K) not in ring_state["released"]:
                break
            src, kt, ncol = item_src(seq[m])
            s = m % RING_K
            dst = ring[:, s, 0:kt * ncol].rearrange("p (k c) -> p k c", k=kt)
            QW.dma(dst, src, writes=[r_slot[s]])
            ring_state["loaded"] += 1

    def ring_get(spec):
        m = ring_state["next"]
        assert seq[m] == spec, (m, seq[m], spec)
        ring_state["next"] += 1
        ring_try_load()
        assert ring_state["loaded"] > m
        _, kt, ncol = item_src(spec)
        s = m % RING_K
        return m, ring[:, s, 0:kt * ncol].rearrange("p (k c) -> p k c", k=kt), r_slot[s]

    def ring_release(m):
        ring_state["released"].add(m)
        ring_try_load()

    def mm(out, lhsT, rhs, start, stop, reads, writes, sig):
        PE.op(lambda e: e.matmul(out, lhsT=lhsT, rhs=rhs, start=start, stop=stop), reads=reads, writes=writes, sig=sig)

    def tr(out, in_, idn, reads, writes, sig=True):
        PE.op(lambda e: e.transpose(out, in_, idn), reads=reads, writes=writes, sig=sig)

    def act(out, in_, func, reads, writes, bias=None, scale=None, accum_out=None):
        kw = {}
        if bias is not None:
            kw["bias"] = bias
        if scale is not None:
            kw["scale"] = scale
        if accum_out is not None:
            kw["accum_out"] = accum_out
        ACT.op(lambda e: e.activation(out=out, in_=in_, func=func, **kw), reads=reads, writes=writes)

    def tt(E, out, in0, in1, op, reads, writes):
        E.op(lambda e: e.tensor_tensor(out=out, in0=in0, in1=in1, op=op), reads=reads, writes=writes)

    def ts(E, out, in0, s1, op0, reads, writes, s2=None, op1=None):
        if op1 is None:
            E.op(lambda e: e.tensor_scalar(out=out, in0=in0, scalar1=s1, scalar2=None, op0=op0), reads=reads, writes=writes)
        else:
            E.op(lambda e: e.tensor_scalar(out=out, in0=in0, scalar1=s1, scalar2=s2, op0=op0, op1=op1), reads=reads, writes=writes)

    def stt(out, in0, scalar, in1, op0, op1, reads, writes):
        DVE.op(lambda e: e.scalar_tensor_tensor(out=out, in0=in0, scalar=scalar, in1=in1, op0=op0, op1=op1), reads=reads, writes=writes)

    def cp(E, out, in_, reads, writes):
        if E is ACT:
            E.op(lambda e: e.copy(out=out, in_=in_), reads=reads, writes=writes)
        else:
            E.op(lambda e: e.tensor_copy(out=out, in_=in_), reads=reads, writes=writes)

    def memset(E, ap, val, writes):
        E.op(lambda e: e.memset(ap, val), writes=writes)

    def block_geom(b):
        tiles = [(j, 128, 128 * j) for j in range(8)]
        chunks = [(0, 512), (512, 512)]
        if b == 1:
            tiles.append((8, 16, 1024))
            chunks.append((1024, 16))
        return tiles, chunks

    QS.dma(cst[:], consts[:], writes=[r_cst])
    QS.dma(pv[:], pvec[:], writes=[r_pv])
    QS.dma(grep1[:], rvec[R_FFN1:R_FFN1 + D].partition_broadcast(128), writes=[r_grep1])
    r_ssmn = Res("ssmn"); r_drep = Res("drep")
    QS.dma(ssmn[:], rvec[R_SSMN:R_SSMN + D].partition_broadcast(128), writes=[r_ssmn])
    QS.dma(drep[:], rvec[R_D:R_D + 16].partition_broadcast(128), writes=[r_drep])
    QW.dma(wdt[:], w_in.rearrange("(k p) c -> p k c", p=128)[:, :, 4608:4624], writes=[r_wdt])
    r_idb = Res("idb"); r_onb = Res("onb"); r_aneg = Res("aneg")
    cp(DVE, idb[:], ident, [r_cst], [r_idb])
    r_negb = Res("negb")
    cp(DVE, negb[:], cst[:, C_NEG:C_NEG + 128], [r_cst], [r_negb])
    memset(DVE, onb[:], 1.0, [r_onb])
    memset(DVE, stat[:], 1.0, [r_stat])
    memset(DVE, stat[:, 48:50], -0.5, [r_stat])
    act(aneg[:], pvc(P_ALOG, 16), AF.Exp, [r_pv], [r_aneg])
    ts(DVE, aneg[:], aneg[:], -1.0, ALU.mult, [r_aneg], [r_aneg])
    memset(POOL, hst[:], 0.0, [r_hst])
    memset(POOL, hstb[:], 0.0, [r_hstb])
    ring_try_load()

    with ExitStack() as ph:
        bufA = P.sb(ph, "bufA", [120, 4, D]); r_bufA = Res("bufA")
        wra = P.sb(ph, "wra", [120, D]); r_wra = Res("wra")
        cs_tm = P.sb(ph, "cs_tm", [16, 1536]); r_cs = Res("cs_tm")
        bufB = P.sb(ph, "bufB", [48, 1536]); r_bufB = Res("bufB")
        wrb = P.sb(ph, "wrb", [48, 1536]); r_wrb = Res("wrb")
        QS.dma(bufA[:], sca.rearrange("(t bl) k c -> (bl k) t c", bl=4), writes=[r_bufA])
        QS.dma(wra[:], warep[:], writes=[r_wra])
        QS.dma(bufB[:], scb.rearrange("b k c -> (b k) c"), writes=[r_bufB])
        QS.dma(wrb[:], wbrep[:], writes=[r_wrb])
        for jj in range(8):
            QS.dma(x_sb[:, jj, :], xp[128 * jj:128 * jj + 128, :], writes=[r_x[jj]])
        P.out_toks.append(QS.dma(nca_s[:, 0:29, :], sca[:, 1:30, :]))
        P.out_toks.append(QS.dma(ncb_s[:, 0:2, :], scb[:, 1:3, :]))
        tt(POOL, bufA[:, 0:2, :], bufA[:, 0:2, :], wra[:].unsqueeze(1).to_broadcast([120, 2, D]), ALU.mult, [r_bufA.sub(0), r_wra], [r_bufA.sub(0)])
        tt(DVE, bufA[:, 2:4, :], bufA[:, 2:4, :], wra[:].unsqueeze(1).to_broadcast([120, 2, D]), ALU.mult, [r_bufA.sub(1), r_wra], [r_bufA.sub(1)])
        tt(POOL, bufB[:], bufB[:], wrb[:], ALU.mult, [r_bufB, r_wrb], [r_bufB])
        for dc in range(2):
            for t in range(4):
                mm(ps[dc][0:16, :], cst[0:120, C_INDA + 16 * t:C_INDA + 16 * t + 16], bufA[:, t, dc * 512:(dc + 1) * 512],
                   t == 0, t == 3, [r_cst, r_bufA], [r_ps[dc]], t == 3)
            cp(ACT, cs_tm[:, dc * 512:(dc + 1) * 512], ps[dc][0:16, :], [r_ps[dc]], [r_cs])
        for c in range(8):
            tr(ps[2][:, 16 * c:16 * c + 16], cs_tm[0:16, 128 * c:128 * c + 128], cst[0:16, C_ID:C_ID + 16], [r_cs, r_cst], [r_ps[2]], c == 7)
        cp(ACT, csAT[:].rearrange("p a b -> p (a b)"), ps[2][:, 0:128], [r_ps[2]], [r_csAT])
        for dc in range(3):
            mm(ps[3 + dc][0:16, :], cst[0:48, C_INDB:C_INDB + 16], bufB[:, dc * 512:(dc + 1) * 512], True, True,
               [r_cst, r_bufB], [r_ps[3 + dc]], True)
            cp(ACT, cs_tm[:, dc * 512:(dc + 1) * 512], ps[3 + dc][0:16, :], [r_ps[3 + dc], r_cs], [r_cs])
        for c in range(12):
            tr(ps[2][:, 16 * c:16 * c + 16], cs_tm[0:16, 128 * c:128 * c + 128], cst[0:16, C_ID:C_ID + 16], [r_cs, r_cst], [r_ps[2]], c == 11)
        cp(ACT, csBT[:].rearrange("p a b -> p (a b)"), ps[2][:, 0:192], [r_ps[2]], [r_csBT])
    P.barrier()

    def load_gain(off):
        QS.dma(grep1[:], rvec[off:off + D].partition_broadcast(128), writes=[r_grep1])

    def tile_rstd(tiles, col0):
        for (j, rows, c0) in tiles:
            act(junk[:rows, :], x_sb[:rows, j, :], AF.Square, [r_x[j]], [r_junk, r_stat],
                accum_out=stat[:rows, col0 + j:col0 + j + 1])
        n = len(tiles)
        act(stat[:, col0:col0 + n], stat[:, col0:col0 + n], AF.Sqrt, [r_stat], [r_stat], bias=EPS, scale=1.0 / D)
        DVE.op(lambda e: e.reciprocal(out=stat[:, col0:col0 + n], in_=stat[:, col0:col0 + n]), reads=[r_stat], writes=[r_stat])

    def emit_norm(b, gain_off):
        tiles, _ = block_geom(b)

        def sq(j, rows, c0):
            act(hstb[:rows, :], x_sb[:rows, j, :], AF.Square, [r_x[j]], [r_hstb, r_stat.sub(j)],
                accum_out=stat[:rows, j:j + 1])

        def rest(j, rows, c0):
            par = j % 2
            ts(POOL, stat[:rows, 16 + j:17 + j], stat[:rows, j:j + 1], 1.0 / D, ALU.mult, [r_stat.sub(j)], [r_stat.sub(16 + j)], s2=EPS, op1=ALU.add)
            tt(POOL, stat[:rows, 16 + j:17 + j], stat[:rows, 16 + j:17 + j], stat[:rows, 48:49], ALU.pow, [r_stat.sub(16 + j), r_stat.sub(48)],
               [r_stat.sub(16 + j)])
            stt(hn[:rows, par, :], x_sb[:rows, j, :], stat[:rows, 16 + j:17 + j], grep1[:rows, :], ALU.mult, ALU.mult,
                [r_x[j], r_stat.sub(16 + j), r_grep1], [r_hn[par]])
            tgt, r_tgt = (psb[:], r_psb) if par == 0 else (ps[5][:].bitcast(BF16), r_ps[5])
            for k in range(8):
                tr(tgt[:, 128 * k:128 * k + rows], hn[:rows, par, 128 * k:128 * k + 128], idb[:rows, :rows],
                   [r_hn[par], r_idb], [r_tgt], k == 7)
            cp(DVE if j % 3 == 2 else ACT, hT[:, :, c0:c0 + rows], tgt.rearrange("p (k t) -> p k t", k=8)[:, :, 0:rows], [r_tgt], [r_hT.sub(j)])

        LAG = 3
        for i, tl in enumerate(tiles):
            sq(*tl)
            if i >= LAG:
                rest(*tiles[i - LAG])
        for tl in tiles[max(0, len(tiles) - LAG):]:
            rest(*tl)

    def emit_ffn(b, which):
        tiles, chunks = block_geom(b)
        with ExitStack() as ph:
            hid = P.sb(ph, "hid", [128, NFT, 1040], BF16); r_hid = [Res(f"hid{f}") for f in range(NFT)]
            sg = P.sb(ph, "sg", [128, 2, 1040]); r_sg = [Res("sg0"), Res("sg1")]
            if which == 1:
                fin_g = P.sb(ph, "fin_g", [128, D]); r_fing = Res("fin_g")
                QS.dma(fin_g[:], rvec[R_FINAL:R_FINAL + D].partition_broadcast(128), writes=[r_fing])
            for q in range(6):
                mg, sG, rG = ring_get(("wg", which, q))
                mu, sU, rU = ring_get(("wu", which, q))
                for fl in range(4 if q < 5 else 2):
                    f = 4 * q + fl
                    par = f % 2
                    tb = 4 + par
                    for (slot, rS, b0, toff) in ((sG, rG, 0, 0), (sU, rU, 2, 16)):
                        for k in range(8):
                            for ci, (c0, n) in enumerate(chunks):
                                if ci < 2:
                                    o = ps[b0 + ci][:, 0:n]; ro = r_ps[b0 + ci]
                                else:
                                    o = ps[tb][:, toff:toff + n]; ro = r_ps[tb]
                                mm(o, slot[:, k, 128 * fl:128 * fl + 128], hT[:, k, c0:c0 + n], k == 0, k == 7,
                                   [rS, r_hT], [ro], k == 7 and ci == len(chunks) - 1)
                    for ci, (c0, n) in enumerate(chunks):
                        gi = ps[ci][:, 0:n] if ci < 2 else ps[tb][:, 0:n]
                        rgi = r_ps[ci] if ci < 2 else r_ps[tb]
                        act(sg[:, par, c0:c0 + n], gi, AF.Silu, [rgi], [r_sg[par].sub(ci)])
                    for ci, (c0, n) in enumerate(chunks):
                        ui = ps[2 + ci][:, 0:n] if ci < 2 else ps[tb][:, 16:16 + n]
                        rui = r_ps[2 + ci] if ci < 2 else r_ps[tb]
                        tt(DVE, hid[:, f, c0:c0 + n], ui, sg[:, par, c0:c0 + n], ALU.mult, [rui, r_sg[par].sub(ci)], [r_hid[f].sub(ci)])
                ring_release(mg)
                ring_release(mu)
            for grp in ((0, 1, 2), (3, 4, 5)):
                got = [ring_get(("wd", which, i)) for i in grp]
                fl_list = []
                for gi, i in enumerate(grp):
                    for fl in range(4 if i < 5 else 2):
                        fl_list.append((4 * i + fl, got[gi][1], got[gi][2], fl))
                for (j, rows, c0) in tiles:
                    for dc in range(2):
                        bk = 2 * (j % 2) + dc
                        for idx, (f, slot, rS, fl) in enumerate(fl_list):
                            mm(ps[bk][:rows, :], hid[:, f, c0:c0 + rows], slot[:, fl, dc * 512:(dc + 1) * 512],
                               idx == 0, idx == len(fl_list) - 1, [r_hid[f], rS], [r_ps[bk]], idx == len(fl_list) - 1)
                        xs_ = x_sb[:rows, j, dc * 512:(dc + 1) * 512]
                        stt(xs_, ps[bk][:rows, :], 0.5, xs_, ALU.mult, ALU.add, [r_ps[bk], r_x[j]], [r_x[j]])
                    if which == 1 and grp[0] == 3:
                        final_tile(b, j, rows, fin_g, r_fing)
                for g in got:
                    ring_release(g[0])
        P.barrier(dma=False)

    def emit_groupA(b):
        tiles, chunks = block_geom(b)
        T = 1040 if b == 1 else 1024
        with ExitStack() as ph:
            ua = P.sb(ph, "ua", [128, 8, 1040]); r_ua = [[Res(f"ua{c}_{h}") for h in range(3)] for c in range(8)]
            yaT = P.sb(ph, "yaT", [128, 8, 1040], BF16); r_yaT = Res("yaT")
            with ExitStack() as ph2:
                uext = P.sb(ph2, "uext", [128, 2, 1054], BF16); r_uext = [Res("uext0"), Res("uext1")]
                sig = P.sb(ph2, "sig", [128, 1040]); r_sig = Res("sig")
                dg = P.sb(ph2, "dg", [128, 2, 31, 128], BF16); r_dg = [Res("dg0"), Res("dg1")]
                utail = P.sb(ph2, "utail", [128, 30]); r_utail = Res("utail")
                tailA = grep1; r_tailA = r_grep1
                ustm = sig; r_ustm = r_sig
                ga_slots = {}

                def ga_glu(c):
                    q, cl = c // 4, c % 4
                    par = c % 2
                    if cl == 0:
                        ga_slots[q] = (ring_get(("wa", q)), ring_get(("wb", q)))
                    (ma, sA, rA), (mb, sB, rB) = ga_slots[q]
                    tt(POOL, dg[:, par, :, :], ident.unsqueeze(1).to_broadcast([128, 31, 128]),
                       pv[:, P_WA + 31 * c:P_WA + 31 * c + 31].unsqueeze(2).to_broadcast([128, 31, 128]), ALU.mult,
                       [r_cst, r_pv], [r_dg[par]])
                    for (slot, rS, b0, toff) in ((sB, rB, 2, 16), (sA, rA, 0, 0)):
                        for k in range(8):
                            for ci, (c0, n) in enumerate(chunks):
                                if ci < 2:
                                    o = ps[b0 + ci][:, 0:n]; ro = r_ps[b0 + ci]
                                else:
                                    o = ps[4][:, toff:toff + n]; ro = r_ps[4]
                                mm(o, slot[:, k, 128 * cl:128 * cl + 128], hT[:, k, c0:c0 + n], k == 0, k == 7,
                                   [rS, r_hT], [ro], k == 7 and ci == len(chunks) - 1)
                        if b0 == 2:
                            for ci, (c0, n) in enumerate(chunks[0:2]):
                                act(sig[:, c0:c0 + n], ps[2 + ci][:, 0:n], AF.Sigmoid, [r_ps[2 + ci]], [r_sig.sub(ci)])
                    if cl == 3:
                        ring_release(ma)
                        ring_release(mb)
                    if len(chunks) > 2:
                        c0, n = chunks[2]
                        act(sig[:, c0:c0 + n], ps[4][:, 16:16 + n], AF.Sigmoid, [r_ps[4]], [r_sig.sub(2)])
                    if b == 0:
                        memset(POOL, uext[:, par, 0:30], 0.0, [r_uext[par]])
                    else:
                        cp(POOL, uext[:, par, 0:30], uhalo[:, c, :], [r_uhalo], [r_uext[par]])
                    for ci, (c0, n) in enumerate(chunks):
                        if ci < 2:
                            tt(DVE, uext[:, par, 30 + c0:30 + c0 + n], ps[ci][:, 0:n], sig[:, c0:c0 + n], ALU.mult,
                               [r_ps[ci], r_sig.sub(ci)], [r_uext[par]])
                        else:
                            tt(DVE, usamp[:, c, :], ps[4][:, 0:16], sig[:, c0:c0 + n], ALU.mult, [r_ps[4], r_sig.sub(2)], [r_usamp])
                    if b == 0:
                        cp(POOL, uhalo[:, c, :], uext[:, par, 1024:1054], [r_uext[par]], [r_uhalo])
                    else:
                        tt(DVE, utail[:, :], ps[1][:, 482:512], sig[:, 994:1024], ALU.mult, [r_ps[1], r_sig.sub(1)], [r_utail])
                        tr(ps[4][0:30, 128:256], utail[:, :], ident, [r_utail, r_cst], [r_ps[4]], True)
                        cp(ACT, tailA[0:30, 128 * c:128 * c + 128], ps[4][0:30, 128:256], [r_ps[4]], [r_tailA])

                def ga_conv(c):
                    par = c % 2
                    for hh in range(2):
                        bkc = 5 + hh
                        for k in range(31):
                            mm(ps[bkc][:, :], dg[:, par, k, :], uext[:, par, 512 * hh + k:512 * hh + k + 512], k == 0, k == 30,
                               [r_dg[par], r_uext[par]], [r_ps[bkc]], k == 30)
                        act(ua[:, c, 512 * hh:512 * hh + 512], ps[bkc][:, :], AF.Identity, [r_ps[bkc], r_pv], [r_ua[c][hh]],
                            bias=pvc(P_BA + c))
                    if b == 1:
                        o = ua[:, c, 1024:1040]
                        ts(DVE, o, csAT[:, c, :], pvc(P_BA + c), ALU.add, [r_csAT, r_pv], [r_ua[c][2]])
                        stt(o, usamp[:, c, :], pvc(P_WA + 31 * c + 30), o, ALU.mult, ALU.add, [r_usamp, r_pv, r_ua[c][2]], [r_ua[c][2]])

                ga_glu(0)
                for c in range(8):
                    if c + 1 < 8:
                        ga_glu(c + 1)
                    ga_conv(c)
                if b == 1:
                    P.out_toks.append(QS.dma(nca_p[:, :], tailA[0:30, :], reads=[r_tailA]))
                    for c in range(8):
                        bk = 5 + c // 4
                        tr(ps[bk][0:16, 128 * (c % 4):128 * (c % 4) + 128], usamp[:, c, :], ident, [r_usamp, r_cst], [r_ps[bk]], True)
                        if c % 4 == 3:
                            cp(ACT, ustm[0:16, 512 * (c // 4):512 * (c // 4) + 512], ps[bk][0:16, :], [r_ps[bk]], [r_ustm])
                    P.out_toks.append(QS.dma(nca_s[:, 29, :], ustm[0:16, 0:D], reads=[r_ustm]))
                P.tap(f"ua{b}", ua[:, :, 0:1024], [128, 8, 1024], [x for c in range(8) for x in r_ua[c]])
            P.barrier()
            with ExitStack() as ph2:
                LNW = 512
                uab = P.sb(ph2, "uab", [128, 8, LNW], BF16); r_uab = Res("uab")
                sqb = P.sb(ph2, "sqb", [128, 8, LNW], BF16); r_sqb = Res("sqb")
                mst = P.sb(ph2, "mst", [128, 3, LNW]); r_mst = Res("mst")
                r_uaall = Res("ua_all")
                lnchunks = [(c0 + o, min(LNW, n - o)) for (c0, n) in chunks for o in range(0, n, LNW)]
                def ln_a(i):
                    c0, n = lnchunks[i]
                    bs = 2 * (i % 2)
                    for c in range(8):
                        cp(POOL, uab[:, c, 0:n], ua[:, c, c0:c0 + n], [r_uaall.sub(c)], [r_uab.sub(c)])
                        if c % 2 == 0:
                            act(sqb[:, c, 0:n], ua[:, c, c0:c0 + n], AF.Square, [r_uaall.sub(c)], [r_sqb.sub(c)])
                        else:
                            tt(POOL, sqb[:, c, 0:n], ua[:, c, c0:c0 + n], ua[:, c, c0:c0 + n], ALU.mult, [r_uaall.sub(c)], [r_sqb.sub(c)])
                    for c in range(8):
                        mm(ps[bs][:, 0:n], onb[:], uab[:, c, 0:n], c == 0, c == 7, [r_onb, r_uab], [r_ps[bs]], c == 7)
                    for c in range(8):
                        mm(ps[bs + 1][:, 0:n], onb[:], sqb[:, c, 0:n], c == 0, c == 7, [r_onb, r_sqb], [r_ps[bs + 1]], c == 7)

                def ln_b(i):
                    c0, n = lnchunks[i]
                    bs = 2 * (i % 2)
                    ts(DVE, mst[:, 0, 0:n], ps[bs][:, 0:n], 1.0 / D, ALU.mult, [r_ps[bs]], [r_mst])
                    tt(DVE, mst[:, 1, 0:n], mst[:, 0, 0:n], mst[:, 0, 0:n], ALU.mult, [r_mst], [r_mst])
                    stt(mst[:, 2, 0:n], ps[bs + 1][:, 0:n], 1.0 / D, mst[:, 1, 0:n], ALU.mult, ALU.subtract, [r_ps[bs + 1], r_mst], [r_mst])
                    act(mst[:, 2, 0:n], mst[:, 2, 0:n], AF.Sqrt, [r_mst], [r_mst], bias=EPS, scale=1.0)
                    DVE.op(lambda e: e.reciprocal(out=mst[:, 2, 0:n], in_=mst[:, 2, 0:n]), reads=[r_mst], writes=[r_mst])
                    for c in range(8):
                        o = ua[:, c, c0:c0 + n]
                        tt(DVE, o, o, mst[:, 0, 0:n], ALU.subtract, [r_uaall.sub(c), r_mst], [r_uaall.sub(c)])
                        tt(DVE, o, o, mst[:, 2, 0:n], ALU.mult, [r_uaall.sub(c), r_mst], [r_uaall.sub(c)])
                        act(yaT[:, c, c0:c0 + n], o, AF.Silu, [r_uaall.sub(c), r_pv], [r_yaT.sub(c)], bias=pvc(P_LNB + c), scale=pvc(P_LNG + c))

                ln_a(0)
                for i in range(len(lnchunks)):
                    if i + 1 < len(lnchunks):
                        ln_a(i + 1)
                    ln_b(i)
            got = [ring_get(("wo", 0)), ring_get(("wo", 1))]
            for (j, rows, c0) in tiles:
                for dc in range(2):
                    bk = 2 * (j % 2) + dc
                    for k in range(8):
                        mm(ps[bk][:rows, :], yaT[:, k, c0:c0 + rows], got[k // 4][1][:, k % 4, dc * 512:(dc + 1) * 512],
                           k == 0, k == 7, [r_yaT, got[k // 4][2]], [r_ps[bk]], k == 7)
                    xs_ = x_sb[:rows, j, dc * 512:(dc + 1) * 512]
                    tt(DVE, xs_, ps[bk][:rows, :], xs_, ALU.add, [r_ps[bk], r_x[j]], [r_x[j]])
            for g in got:
                ring_release(g[0])
        P.barrier()

    def emit_groupB(b):
        subs = [(0, 512, False), (512, 512, False)]
        if b == 1:
            subs.append((1024, 16, True))
        for (s0, S, is_s) in subs:
            with ExitStack() as ph:
                nt = 4 if not is_s else 1
                rows = 128 if not is_s else 16
                xs_tm = P.sb(ph, "xs_tm", [128, nt, D]); r_xstm = [Res(f"xstm{t}") for t in range(4)]
                SS = 512 if not is_s else 16
                BT = P.sb(ph, "BT", [128, 2, SS], BF16); r_BT = Res("BT")
                CT = P.sb(ph, "CT", [128, 2, SS], BF16); r_CT = Res("CT")
                Btm = P.sb(ph, "Btm", [128, nt, 256], BF16); r_Btm = Res("Btm")
                dta = P.sb(ph, "dta", [128, nt, 32]); r_dta = Res("dta")
                dtT = P.sb(ph, "dtT", [16, 4, SS] if is_s else [16, 3, SS]); r_dtT = Res("dtT")
                xraw = P.sb(ph, "xraw", [128, 2, SS + 3]); r_xraw = [Res("xraw0"), Res("xraw1")]
                xc = P.sb(ph, "xc", [128, 2, SS]); r_xc = [Res("xc0"), Res("xc1")]
                xsT = xc; r_xsT = r_xc
                y2 = P.sb(ph, "y_sb", [128, 2, D]); r_y2 = [Res("y0"), Res("y1")]
                y_sb = y2[:, 0, :]; r_y = r_y2[0]
                zs2 = P.sb(ph, "zs_sb", [128, 2, D]); r_zs2 = [Res("zs0"), Res("zs1")]
                zs_sb = zs2[:, 0, :]; r_zs = r_zs2[0]
                gn = hn[:, 0, :]; r_gn = Res("gn")
                ysT = P.sb(ph, "ysT", [128, 8, 128], BF16); r_ysT = Res("ysT")
                xs_sT = P.sb(ph, "xs_sT", [128, 8, 16]); r_xssT = Res("xs_sT")
                BCs = P.sb(ph, "BCs", [128, 4, 16]); r_BCs = Res("BCs")

                b1_slots = {}

                def b1_mm(f):
                    i, fl = f // 4, f % 4
                    if fl == 0:
                        b1_slots[i] = ring_get(("wx", i))
                    mx, sX, rX = b1_slots[i]
                    bk = f % 2
                    for k in range(8):
                        mm(ps[bk][:, 0:S], sX[:, k, 128 * fl:128 * fl + 128], hT[:, k, s0:s0 + S], k == 0, k == 7,
                           [rX, r_hT], [r_ps[bk]], k == 7)
                    if fl == 3:
                        ring_release(mx)

                def b1_a(f):
                    par = f % 2
                    bk = f % 2
                    wk = lambda kk: pvc(P_WB + 4 * f + kk)
                    if not is_s:
                        cp(ACT, xraw[:, par, 3:3 + S], ps[bk][:, 0:S], [r_ps[bk]], [r_xraw[par]])
                        if b == 0 and s0 == 0:
                            memset(POOL, xraw[:, par, 0:3], 0.0, [r_xraw[par]])
                        else:
                            cp(POOL, xraw[:, par, 0:3], bhalo[:, f, :], [r_bhalo], [r_xraw[par]])
                        cp(POOL, bhalo[:, f, :], xraw[:, par, S:S + 3], [r_xraw[par]], [r_bhalo])
                        o = xc[:, par, 0:S]
                        ts(DVE, o, xraw[:, par, 0:S], wk(0), ALU.mult, [r_xraw[par], r_pv], [r_xc[par]], s2=pvc(P_BB + f), op1=ALU.add)
                        for kk in range(1, 4):
                            stt(o, xraw[:, par, kk:kk + S], wk(kk), o, ALU.mult, ALU.add, [r_xraw[par], r_pv, r_xc[par]], [r_xc[par]])
                    else:
                        cp(ACT, xrs[:, f, :], ps[bk][:, 0:16], [r_ps[bk]], [r_xrs])
                        o = xc[:, par, 0:16]
                        ts(DVE, o, csBT[:, f, :], pvc(P_BB + f), ALU.add, [r_csBT, r_pv], [r_xc[par]])
                        stt(o, xrs[:, f, :], wk(3), o, ALU.mult, ALU.add, [r_xrs, r_pv, r_xc[par]], [r_xc[par]])

                def b1_b(f):
                    par = f % 2
                    if not is_s:
                        o = xc[:, par, 0:S]
                        if f < 8:
                            act(xsT[:, par, 0:S], o, AF.Silu, [r_xc[par]], [r_xsT[par]])
                            bk2 = 2 + f % 2
                            for t in range(4):
                                tr(ps[bk2][:, 128 * t:128 * t + 128], xsT[:, par, 128 * t:128 * t + 128], ident,
                                   [r_xsT[par], r_cst], [r_ps[bk2]], t == 3)
                            cp(ACT, xs_tm[:, :, 128 * f:128 * f + 128],
                               ps[bk2][:].rearrange("p (t c) -> p t c", t=4), [r_ps[bk2]], r_xstm)
                        elif f < 10:
                            g = f - 8
                            act(BT[:, g, 0:S], o, AF.Silu, [r_xc[par]], [r_BT])
                            for t in range(4):
                                tr(psb[:, 128 * t:128 * t + 128], BT[:, g, 128 * t:128 * t + 128], idb[:], [r_BT, r_idb], [r_psb], t == 3)
                            cp(ACT, Btm[:, :, 128 * g:128 * g + 128], psb[:, 0:512].rearrange("p (t c) -> p t c", t=4), [r_psb], [r_Btm])
                        else:
                            g = f - 10
                            act(CT[:, g, 0:S], o, AF.Silu, [r_xc[par]], [r_CT])
                    else:
                        o = xc[:, par, 0:16]
                        if f < 8:
                            act(xs_sT[:, f, :], o, AF.Silu, [r_xc[par]], [r_xssT])
                        else:
                            act(BCs[:, f - 8, :], o, AF.Silu, [r_xc[par]], [r_BCs])

                for k in range(8):
                    mm(ps[4][0:16, 0:S], wdt[:, k, :], hT[:, k, s0:s0 + S], k == 0, k == 7, [r_wdt, r_hT], [r_ps[4]], k == 7)
                xb_, tm_, dt_ = (dtT[:, i, 0:S] for i in range(3))
                a_ = dtT[:, 3, 0:S] if is_s else xb_
                A_IDX = 3 if is_s else 0
                act(xb_, ps[4][0:16, 0:S], AF.Identity, [r_ps[4], r_pv], [r_dtT], bias=pvc(P_DTB, 16))
                act(tm_, xb_, AF.Abs, [r_dtT], [r_dtT])
                act(tm_, tm_, AF.Exp, [r_dtT], [r_dtT], scale=-1.0)
                act(tm_, tm_, AF.Ln, [r_dtT], [r_dtT], bias=1.0)
                ts(DVE, dt_, xb_, 0.0, ALU.max, [r_dtT], [r_dtT])
                tt(DVE, dt_, dt_, tm_, ALU.add, [r_dtT], [r_dtT])
                ts(DVE, a_, dt_, aneg[:, 0:1], ALU.mult, [r_dtT, r_aneg], [r_dtT])
                b1_mm(0)
                b1_a(0)
                for f in range(12):
                    if f + 1 < 12:
                        b1_mm(f + 1)
                        b1_a(f + 1)
                    b1_b(f)
                if not is_s:
                    for t in range(4):
                        tr(ps[5][:, 32 * t:32 * t + 16], dtT[:, 2, 128 * t:128 * t + 128], cst[0:16, C_ID:C_ID + 16], [r_dtT, r_cst], [r_ps[5]], False)
                        tr(ps[5][:, 32 * t + 16:32 * t + 32], dtT[:, A_IDX, 128 * t:128 * t + 128], cst[0:16, C_ID:C_ID + 16], [r_dtT, r_cst], [r_ps[5]], t == 3)
                    cp(ACT, dta[:].rearrange("p a b -> p (a b)"), ps[5][:, 0:128], [r_ps[5]], [r_dta])
                if b == 1 and s0 == 512:
                    for f in range(12):
                        bk = 5 + (f // 4) % 2
                        tr(ps[bk][0:3, 128 * (f % 4):128 * (f % 4) + 128], bhalo[:, f, :], ident, [r_bhalo, r_cst], [r_ps[bk]], True)
                        if f % 4 == 3:
                            dstb, rdst = (zs_sb[0:3, 512 * (f // 4):512 * (f // 4) + 512], r_zs) if f < 8 else (y_sb[0:3, 0:512], r_y)
                            cp(ACT, dstb, ps[bk][0:3, :], [r_ps[bk]], [rdst])
                    P.out_toks.append(QS.dma(ncb_p[:, 0:1024], zs_sb[0:3, :], reads=[r_zs]))
                    P.out_toks.append(QS.dma(ncb_p[:, 1024:1536], y_sb[0:3, 0:512], reads=[r_y]))
                if is_s:
                    for f in range(12):
                        bk = 5 + (f // 4) % 2
                        tr(ps[bk][0:16, 128 * (f % 4):128 * (f % 4) + 128], xrs[:, f, :], ident, [r_xrs, r_cst], [r_ps[bk]], True)
                        if f % 4 == 3:
                            dstb, rdst = (zs_sb[0:16, 512 * (f // 4):512 * (f // 4) + 512], r_zs) if f < 8 else (y_sb[0:16, 0:512], r_y)
                            cp(ACT, dstb, ps[bk][0:16, :], [r_ps[bk]], [rdst])
                    P.out_toks.append(QS.dma(ncb_s[:, 2, 0:1024], zs_sb[0:16, :], reads=[r_zs]))
                    P.out_toks.append(QS.dma(ncb_s[:, 2, 1024:1536], y_sb[0:16, 0:512], reads=[r_y]))

                ckpt(f"b1_{b}_{s0}")
                gz = [ring_get(("wz", 0)), ring_get(("wz", 1))]
                go = [ring_get(("wo", 2)), ring_get(("wo", 3))]

                def zproj(rows, c0, zp):
                    for dc in range(2):
                        for k in range(8):
                            mm(ps[2 + dc][:rows, :], hT[:, k, c0:c0 + rows], gz[dc][1][:, k, :], k == 0, k == 7,
                               [r_hT, gz[dc][2]], [r_ps[2 + dc]], k == 7)
                        act(zs2[:rows, zp, 512 * dc:512 * dc + 512], ps[2 + dc][:rows, :], AF.Silu, [r_ps[2 + dc]], [r_zs2[zp].sub(dc)])

                def post1(rows, r_xs_t, xs_t, zp, yp):
                    zs_sb = zs2[:, zp, :]
                    r_zs = r_zs2[zp]
                    y_sb = y2[:, yp, :]
                    r_y = r_y2[yp]
                    tt(DVE, y_sb[:rows, :], y_sb[:rows, :], xs_t, ALU.add, [r_y, r_xs_t], [r_y])
                    tt(DVE, y_sb[:rows, :], y_sb[:rows, :], zs_sb[:rows, :], ALU.mult, [r_y, r_zs], [r_y])
                    for g in range(2):
                        act(junk[:rows, 512 * g:512 * g + 512], y_sb[:rows, 512 * g:512 * g + 512], AF.Square, [r_y], [r_junk.sub(g), r_stat.sub(32 + g)],
                            accum_out=stat[:rows, 32 + g:33 + g])
                    ts(POOL, stat[:rows, 34:36], stat[:rows, 32:34], 1.0 / 512, ALU.mult, [r_stat.sub(32), r_stat.sub(33)], [r_stat.sub(34)],
                       s2=EPS, op1=ALU.add)
                    tt(POOL, stat[:rows, 36:38], stat[:rows, 34:36], stat[:rows, 48:50], ALU.pow, [r_stat.sub(34), r_stat.sub(48)], [r_stat.sub(36)])
                    for g in range(2):
                        stt(gn[:rows, 512 * g:512 * g + 512], y_sb[:rows, 512 * g:512 * g + 512], stat[:rows, 36 + g:37 + g],
                            ssmn[:rows, 512 * g:512 * g + 512], ALU.mult, ALU.mult, [r_y, r_stat.sub(36), r_ssmn], [r_gn.sub(g)])

                def post2(rows):
                    for k in range(8):
                        tr(psb[:, 128 * k:128 * k + rows], gn[:rows, 128 * k:128 * k + 128], idb[:rows, :rows], [r_gn, r_idb], [r_psb], k == 7)
                    cp(ACT, ysT[:, :, 0:rows], psb[:].rearrange("p (k t) -> p k t", k=8)[:, :, 0:rows], [r_psb], [r_ysT])

                def post3(rows, j):
                    for dc in range(2):
                        for k in range(8):
                            mm(ps[2 + dc][:rows, :], ysT[:, k, 0:rows], go[k // 4][1][:, k % 4, 512 * dc:512 * dc + 512], k == 0, k == 7,
                               [r_ysT, go[k // 4][2]], [r_ps[2 + dc]], k == 7)
                        xs_ = x_sb[:rows, j, dc * 512:(dc + 1) * 512]
                        tt(DVE, xs_, ps[2 + dc][:rows, :], xs_, ALU.add, [r_ps[2 + dc], r_x[j]], [r_x[j]])

                if not is_s:
                    E = P.sb(ph, "E", [128, 1, 8, 128]); r_E = [Res("E0")] * 2
                    CBm = P.sb(ph, "CBm", [128, 2, 128]); r_CBm = Res("CBm")
                    MT = P.sb(ph, "MT", [128, 2, 16, 128], BF16); r_MT = [Res("MT0"), Res("MT1")]
                    Xb = P.sb(ph, "Xb", [128, 2, D], BF16); r_Xb = [Res("Xb0"), Res("Xb1")]
                    Xd = P.sb(ph, "Xd", [128, 2, D], BF16); r_Xd = [Res("Xd0"), Res("Xd1")]
                    sm = P.sb(ph, "sm", [128, 2, 8, 16]); r_sm = [Res("sm0"), Res("sm1")]

                    def smv(t):
                        pp = t % 2
                        return [sm[:, pp, i, :] for i in range(8)], r_sm[pp], pp

                    def f_pe(t):
                        a_tm = dta[:, t, 16:32]
                        cols = slice(128 * t, 128 * t + 128)
                        mm(ps[6][:, 0:16], tri, a_tm, True, True, [r_cst, r_dta], [r_ps[6]], False)
                        for g in range(2):
                            mm(ps[6][:, 128 + 128 * g:256 + 128 * g], BT[:, g, cols], CT[:, g, cols], True, True, [r_BT, r_CT], [r_ps[6]], g == 1)
                        f_arow(t, 0)
                        zproj(128, s0 + 128 * t, t % 2)

                    def f_arow(t, g):
                        for h8 in range(8):
                            h = 8 * g + h8
                            o = ps[h8 // 4][:, 128 * (h8 % 4):128 * (h8 % 4) + 128]
                            mm(o, dta[:, t, 16 + h:17 + h].to_broadcast([128, 128]), tri, True, False, [r_dta, r_cst], [r_ps[h8 // 4]], False)
                            mm(o, idb[:], negb[:], False, True, [r_idb, r_negb], [r_ps[h8 // 4]], h8 % 4 == 3)

                    def f_a(t):
                        (acs, nacs, e2, cdr, dte, dtd, tmp16, alast), rsm, pp = smv(t)
                        cp(ACT, acs, ps[6][:, 0:16], [r_ps[6]], [rsm.sub("acs")])
                        ACT.op(lambda e: e.mul(out=nacs, in_=acs, mul=-1.0), reads=[rsm.sub("acs")], writes=[rsm.sub("nacs")])
                        cp(ACT, CBm[:].rearrange("p g i -> p (g i)"), ps[6][:, 128:384], [r_ps[6]], [r_CBm])

                    def f_g(t, g):
                        (acs, nacs, e2, cdr, dte, dtd, tmp16, alast), rsm, pp = smv(t)
                        for q2 in range(2):
                            lastc = ps[q2][:].rearrange("p (h i) -> p h i", h=4)[:, :, 127]
                            q4 = 2 * g + q2
                            cp(ACT, alast[:, 4 * q4:4 * q4 + 4], lastc, [r_ps[q2]], [rsm.sub(f"alast{q4}")])
                        for h8 in range(8):
                            h = 8 * g + h8
                            act(E[:, 0, h8, :], ps[h8 // 4][:, 128 * (h8 % 4):128 * (h8 % 4) + 128], AF.Exp, [r_ps[h8 // 4], rsm.sub("nacs")], [r_E[0].sub(h8)],
                                bias=sm[:, pp, 1, h:h + 1])
                        if g == 0:
                            f_arow(t, 1)
                        tt(DVE, MT[:, pp, 8 * g:8 * g + 8, :], E[:, 0, :, :], CBm[:, g:g + 1, :].to_broadcast([128, 8, 128]), ALU.mult,
                           [r_E[0], r_CBm], [r_MT[pp].sub(g)])

                    def f_tail(t):
                        (acs, nacs, e2, cdr, dte, dtd, tmp16, alast), rsm, pp = smv(t)
                        dt_tm = dta[:, t, 0:16]
                        r_al = [rsm.sub(f"alast{q4}") for q4 in range(4)]
                        act(e2, acs, AF.Exp, [rsm.sub("acs")], [rsm.sub("e2")])
                        act(cdr, alast, AF.Exp, r_al, [rsm.sub("cdr")])
                        tt(DVE, tmp16, alast, acs, ALU.subtract, r_al + [rsm.sub("acs")], [rsm.sub("tmp")])
                        act(dte, tmp16, AF.Exp, [rsm.sub("tmp")], [rsm.sub("dte")])
                        tt(DVE, dtd, dt_tm, dte, ALU.mult, [r_dta, rsm.sub("dte")], [rsm.sub("dtd")])
                        xs3 = xs_tm[:, t, :].rearrange("p (h d) -> p h d", h=16)
                        tt(POOL, Xb[:, pp, :].rearrange("p (h d) -> p h d", h=16), xs3, dt_tm.unsqueeze(2).to_broadcast([128, 16, 64]), ALU.mult,
                           [r_xstm[t], r_dta], [r_Xb[pp]])
                        tt(POOL, Xd[:, pp, :].rearrange("p (h d) -> p h d", h=16), xs3, dtd.unsqueeze(2).to_broadcast([128, 16, 64]), ALU.mult,
                           [r_xstm[t], rsm.sub("dtd")], [r_Xd[pp]])
                        tt(POOL, xs3, xs3, drep[:].unsqueeze(2).to_broadcast([128, 16, 64]), ALU.mult, [r_xstm[t], r_drep], [r_xstm[t]])

                    def head1(t):
                        (acs, nacs, e2, cdr, dte, dtd, tmp16, alast), rsm, pp = smv(t)
                        cols = slice(128 * t, 128 * t + 128)
                        for g in range(2):
                            mm(ps[4 + g][:, :], CT[:, g, cols], hstb[:, 512 * g:512 * g + 512], True, True, [r_CT, r_hstb], [r_ps[4 + g]], True)
                        for g in range(2):
                            yv = y2[:, pp, 512 * g:512 * g + 512].rearrange("p (h d) -> p h d", h=8)
                            tt(DVE, yv, ps[4 + g][:].rearrange("p (h d) -> p h d", h=8), e2[:, 8 * g:8 * g + 8].unsqueeze(2).to_broadcast([128, 8, 64]),
                               ALU.mult, [r_ps[4 + g], rsm.sub("e2")], [r_y2[pp].sub(g)])

                    def head2(t):
                        (acs, nacs, e2, cdr, dte, dtd, tmp16, alast), rsm, pp = smv(t)
                        for h in range(16):
                            mm(ps[4 + h // 8][:, 64 * (h % 8):64 * (h % 8) + 64], MT[:, pp, h, :], Xb[:, pp, 64 * h:64 * h + 64], True, True,
                               [r_MT[pp], r_Xb[pp]], [r_ps[4 + h // 8]], h % 8 == 7)
                        for g in range(2):
                            tt(DVE, y2[:, pp, 512 * g:512 * g + 512], y2[:, pp, 512 * g:512 * g + 512], ps[4 + g][:, :], ALU.add,
                               [r_y2[pp].sub(g), r_ps[4 + g]], [r_y2[pp].sub(g)])

                    def head3(t):
                        (acs, nacs, e2, cdr, dte, dtd, tmp16, alast), rsm, pp = smv(t)
                        h3 = hst[:].rearrange("p (h d) -> p h d", h=16)
                        tt(DVE, h3, h3, cdr.unsqueeze(2).to_broadcast([128, 16, 64]), ALU.mult, [r_hst, rsm.sub("cdr")], [r_hst])
                        for g in range(2):
                            mm(ps[4 + g][:, :], Btm[:, t, 128 * g:128 * g + 128], Xd[:, pp, 512 * g:512 * g + 512], True, True,
                               [r_Btm, r_Xd[pp]], [r_ps[4 + g]], True)
                        for g in range(2):
                            tt(DVE, hst[:, 512 * g:512 * g + 512], hst[:, 512 * g:512 * g + 512], ps[4 + g][:, :], ALU.add, [r_hst, r_ps[4 + g]], [r_hst])
                        cp(ACT, hstb[:], hst[:], [r_hst], [r_hstb])

                    cp(ACT, hstb[:], hst[:], [r_hst], [r_hstb])
                    f_pe(0)
                    f_a(0)
                    f_g(0, 0)
                    f_g(0, 1)
                    f_tail(0)
                    for t in range(5):
                        hd = t < 4
                        fr = t + 1 < 4
                        po = t >= 1
                        if po:
                            post1(128, r_xstm[t - 1], xs_tm[:, t - 1, :], (t - 1) % 2, (t - 1) % 2)
                        if hd:
                            head1(t)
                        if fr:
                            f_pe(t + 1)
                        if hd:
                            head2(t)
                        if po:
                            post2(128)
                        if fr:
                            f_a(t + 1)
                            f_g(t + 1, 0)
                        if hd:
                            head3(t)
                        if po:
                            post3(128, s0 // 128 + t - 1)
                        if fr:
                            f_g(t + 1, 1)
                            f_tail(t + 1)
                    ckpt(f"b2_{b}_{s0}")
                    if b == 1 and s0 == 512:
                        ho = E[:, 0, :, :]; r_ho = r_E[0]
                        for jj in range(8):
                            bk = 5 + (jj // 4) % 2
                            tr(ps[bk][:, 128 * (jj % 4):128 * (jj % 4) + 128], hst[:, 128 * jj:128 * jj + 128], ident, [r_hst, r_cst], [r_ps[bk]], True)
                            if jj % 4 == 3:
                                cp(ACT, ho[:, 4 * (jj // 4):4 * (jj // 4) + 4, :], ps[bk][:].rearrange("p (a n) -> p a n", a=4), [r_ps[bk]], [r_ho])
                        P.out_toks.append(QS.dma(nss_p.rearrange("(j m) n -> m j n", m=128), ho, reads=[r_ho]))
                else:
                    dec = P.sb(ph, "dec", [128, 8, 16]); r_dec = Res("dec")
                    dtx = P.sb(ph, "dtx", [128, 8, 16]); r_dtx = Res("dtx")
                    ysT_s = P.sb(ph, "ysT_s", [128, 8, 16]); r_ysTs = Res("ysT_s")
                    h0 = P.sb(ph, "h0", [128, 2, 8, 128]); r_h0 = [Res("h0a"), Res("h0b")]
                    h1 = P.sb(ph, "h1", [128, 2, 8, 128]); r_h1 = [Res("h1a"), Res("h1b")]
                    t2 = P.sb(ph, "t2", [128, 2, 8, 128]); r_t2 = [Res("t2a"), Res("t2b")]
                    t3 = P.sb(ph, "t3", [128, 2, 8, 128]); r_t3 = [Res("t3a"), Res("t3b")]
                    ckpt("s_pre")
                    for jj in range(8):
                        ex = cst[0:16, C_EXP + 128 * jj:C_EXP + 128 * jj + 128]
                        mm(ps[6][:, 16 * jj:16 * jj + 16], ex, dtT[:, 2, 0:16], True, True, [r_cst, r_dtT], [r_ps[6]], False)
                        mm(ps[6][:, 128 + 16 * jj:128 + 16 * jj + 16], ex, dtT[:, 3, 0:16], True, True, [r_cst, r_dtT], [r_ps[6]], jj == 7)
                    ckpt("s_mm")
                    act(dec[:].rearrange("p a b -> p (a b)"), ps[6][:, 128:256], AF.Exp, [r_ps[6]], [r_dec])
                    ckpt("s_act")
                    tt(DVE, dtx[:].rearrange("p a b -> p (a b)"), ps[6][:, 0:128], xs_sT[:].rearrange("p a b -> p (a b)"), ALU.mult,
                       [r_ps[6], r_xssT], [r_dtx])
                    ckpt("s_exp")
                    for bb in range(16):
                        par = bb % 2
                        bk = bb % 4
                        ckpt(f"s_b{bb}")
                        if bb == 0:
                            for b2 in range(2):
                                QS.dma(h0[:, b2, :, :], sss[b2].rearrange("(j m) n -> m j n", m=128), writes=[r_h0[b2]])
                        for g in range(2):
                            mm(ps[bk][:, 128 * g:128 * g + 128], BCs[:, g, bb:bb + 1].to_broadcast([128, 128]), ident, True, True,
                               [r_BCs, r_cst], [r_ps[bk]], False)
                            mm(ps[bk][:, 256 + 128 * g:384 + 128 * g], BCs[:, 2 + g, bb:bb + 1].to_broadcast([128, 128]), ident, True, True,
                               [r_BCs, r_cst], [r_ps[bk]], g == 1)
                        tt(POOL, h1[:, par, :, :], h0[:, par, :, :], dec[:, :, bb:bb + 1].to_broadcast([128, 8, 128]), ALU.mult,
                           [r_h0[par], r_dec], [r_h1[par]])
                        if bb + 2 < 16:
                            QS.dma(h0[:, par, :, :], sss[bb + 2].rearrange("(j m) n -> m j n", m=128), writes=[r_h0[par]])
                        for g in range(2):
                            tt(DVE, t2[:, par, 4 * g:4 * g + 4, :], ps[bk][:, 128 * g:128 * g + 128].unsqueeze(1).to_broadcast([128, 4, 128]),
                               dtx[:, 4 * g:4 * g + 4, bb:bb + 1].to_broadcast([128, 4, 128]), ALU.mult, [r_ps[bk], r_dtx], [r_t2[par].sub(g)])
                        tt(POOL, h1[:, par, :, :], h1[:, par, :, :], t2[:, par, :, :], ALU.add, [r_h1[par], r_t2[par]], [r_h1[par]])
                        P.out_toks.append(QS.dma(nss_s[bb].rearrange("(j m) n -> m j n", m=128), h1[:, par, :, :], reads=[r_h1[par]]))
                        for g in range(2):
                            tt(DVE, t3[:, par, 4 * g:4 * g + 4, :], h1[:, par, 4 * g:4 * g + 4, :],
                               ps[bk][:, 256 + 128 * g:384 + 128 * g].unsqueeze(1).to_broadcast([128, 4, 128]), ALU.mult,
                               [r_h1[par], r_ps[bk]], [r_t3[par].sub(g)])
                        DVE.op(lambda e: e.tensor_reduce(out=ysT_s[:, :, bb], in_=t3[:, par, :, :], axis=AX.X, op=ALU.add), reads=[r_t3[par]], writes=[r_ysTs.sub(bb)])
                    ckpt("s_loop")
                    for jj in range(8):
                        bk = 5 + (jj // 4) % 2
                        tr(ps[bk][0:16, 128 * (jj % 4):128 * (jj % 4) + 128], ysT_s[:, jj, :], ident, [r_ysTs, r_cst], [r_ps[bk]], True)
                        if jj % 4 == 3:
                            cp(ACT, y_sb[0:16, 512 * (jj // 4):512 * (jj // 4) + 512], ps[bk][0:16, :], [r_ps[bk]], [r_y])
                    for jj in range(8):
                        bk = 5 + (jj // 4) % 2
                        tr(ps[bk][0:16, 128 * (jj % 4):128 * (jj % 4) + 128], xs_sT[:, jj, :], ident, [r_xssT, r_cst], [r_ps[bk]], True)
                        if jj % 4 == 3:
                            cp(ACT, xs_tm[0:16, 0, 512 * (jj // 4):512 * (jj // 4) + 512], ps[bk][0:16, :], [r_ps[bk]], [r_xstm[0]])
                    xs3 = xs_tm[0:16, 0, :].rearrange("p (h d) -> p h d", h=16)
                    tt(POOL, xs3, xs3, drep[0:16, :].unsqueeze(2).to_broadcast([16, 16, 64]), ALU.mult, [r_xstm[0], r_drep], [r_xstm[0]])
                    ckpt("s_post")
                    zproj(16, 1024, 0)
                    post1(16, r_xstm[0], xs_tm[0:16, 0, :], 0, 0)
                    post2(16)
                    post3(16, 8)
                for g in gz + go:
                    ring_release(g[0])
            P.barrier()

    def final_tile(b, j, rows, gain, r_gain):
        act(hstb[:rows, :], x_sb[:rows, j, :], AF.Square, [r_x[j]], [r_hstb, r_stat.sub(j)], accum_out=stat[:rows, j:j + 1])
        ts(POOL, stat[:rows, 16 + j:17 + j], stat[:rows, j:j + 1], 1.0 / D, ALU.mult, [r_stat.sub(j)], [r_stat.sub(16 + j)], s2=EPS, op1=ALU.add)
        tt(POOL, stat[:rows, 16 + j:17 + j], stat[:rows, 16 + j:17 + j], stat[:rows, 48:49], ALU.pow, [r_stat.sub(16 + j), r_stat.sub(48)],
           [r_stat.sub(16 + j)])
        stt(x_sb[:rows, j, :], x_sb[:rows, j, :], stat[:rows, 16 + j:17 + j], gain[:rows, :], ALU.mult, ALU.mult,
            [r_x[j], r_stat.sub(16 + j), r_gain], [r_x[j]])
        if j < 8:
            dst = yp[1024 * b + 128 * j:1024 * b + 128 * j + 128, :]
        else:
            dst = ysm[:, :]
        P.out_toks.append(QS.dma(dst, x_sb[:rows, j, :], reads=[r_x[j]]))
        if b == 0:
            QS.dma(x_sb[:, j, :], xp[1024 + 128 * j:1024 + 128 * j + 128, :], writes=[r_x[j]])

    def emit_final(b):
        pass

    def load_x(b):
        if b == 0:
            return
        QS.dma(x_sb[0:16, 8, :], xsm[:, :], writes=[r_x[8]])

    def xtap(name, b):
        tiles, _ = block_geom(b)
        P.tap(name, x_sb[:, 0:8, :], [128, 8, D], r_x[0:8])

    stage_no = [0]

    def stage(fn, *a):
        if stop_after is not None and stage_no[0] >= stop_after:
            raise _Stop()
        stage_no[0] += 1
        fn(*a)

    try:
        for b in range(2):
            stage(load_x, b)
            stage(emit_norm, b, R_FFN1)
            load_gain(R_MIX)
            P.tap(f"h1T{b}", hT[:, :, 0:1024], [128, 8, 1024], [r_hT])
            stage(emit_ffn, b, 0)
            xtap(f"x1_{b}", b)
            stage(emit_norm, b, R_MIX)
            stage(emit_groupA, b)
            xtap(f"xa_{b}", b)
            load_gain(R_FFN2)
            stage(emit_groupB, b)
            xtap(f"xb_{b}", b)
            stage(emit_norm, b, R_FFN2)
            if b == 0:
                load_gain(R_FFN1)
            stage(emit_ffn, b, 1)
            xtap(f"x2_{b}", b)
            stage(emit_final, b)
    except _Stop:
        pass

    for t in P.out_toks:
        SP.wait(t)
    assert stop_after is not None or stop_at is not None or ring_state["next"] == len(seq)
    P.stats = {e.name: (e.nops, e.nwaits, e.cnt) for e in P.engs}
    return P


def _consts():
    c = np.zeros((128, NCONST), np.float32)
    c[:, C_ID:C_ID + 128] = np.eye(128, dtype=np.float32)
    c[:, C_TRI:C_TRI + 128] = np.triu(np.ones((128, 128), np.float32))
    for j in range(8):
        for m in range(128):
            c[2 * j + m // 64, C_EXP + 128 * j + m] = 1.0
    for t in range(4):
        for bl in range(4):
            for k in range(30):
                c[bl * 30 + k, C_INDA + 16 * t + 4 * t + bl] = 1.0
    for bq in range(16):
        for k in range(3):
            c[bq * 3 + k, C_INDB + bq] = 1.0
    c[:, C_NEG:C_NEG + 128] = -16384.0 * np.tril(np.ones((128, 128), np.float32), -1)
    return c


def _pvec(inp):
    pv = np.zeros((128, NPV), np.float32)
    wa = inp["conv_dw_w"][0]
    pv[:, P_WA:P_WA + 248] = wa.T.reshape(8, 128, 31).transpose(1, 0, 2).reshape(128, 248)
    for off, key in ((P_BA, "conv_dw_b"), (P_LNG, "conv_ln_g"), (P_LNB, "conv_ln_b")):
        pv[:, off:off + 8] = inp[key][0].reshape(8, 128).T
    wb = inp["ssm_conv_w"][0]
    pv[:, P_WB:P_WB + 48] = wb.T.reshape(12, 128, 4).transpose(1, 0, 2).reshape(128, 48)
    pv[:, P_BB:P_BB + 12] = inp["ssm_conv_b"][0].reshape(12, 128).T
    pv[0:16, P_DTB] = inp["ssm_dt_bias"][0]
    pv[0:16, P_ALOG] = inp["ssm_a_log"][0]
    return pv


_PROG = {}


def _get_prog(taps=(), stop_after=None, stop_at=None):
    key = (tuple(taps), stop_after, stop_at)
    if key not in _PROG:
        _PROG[key] = build_program(taps, stop_after, stop_at)
    return _PROG[key]


def kernel(_taps=(), _stop_after=None, _cores=8, _stop_at=None, **inp):
    inp = {k: np.asarray(v) for k, v in inp.items()}
    P = _get_prog(_taps, _stop_after, _stop_at)
    f32 = lambda a: np.ascontiguousarray(a, dtype=np.float32)
    consts = _consts()
    pvec = _pvec(inp)
    rvec = f32(np.concatenate([inp["ffn1_norm"][0], inp["mix_norm"][0], inp["ffn2_norm"][0], inp["final_norm"],
                               inp["ssm_norm"][0], inp["ssm_d"][0]]))
    warep = f32(np.tile(inp["conv_dw_w"][0][0:30], (4, 1)))
    wbrep = f32(np.tile(inp["ssm_conv_w"][0][0:3], (16, 1)))
    shared = {
        "w1g": f32(inp["ffn1_w_gate"][0]), "w1u": f32(inp["ffn1_w_up"][0]), "w1d": f32(inp["ffn1_w_down"][0]),
        "w2g": f32(inp["ffn2_w_gate"][0]), "w2u": f32(inp["ffn2_w_up"][0]), "w2d": f32(inp["ffn2_w_down"][0]),
        "w_in": f32(inp["w_in"][0]), "w_out": f32(inp["w_out"][0]),
        "consts": consts, "pvec": pvec, "rvec": rvec, "warep": warep, "wbrep": wbrep,
    }
    in_maps = []
    for c in range(_cores):
        sl = slice(16 * c, 16 * c + 16)
        m = dict(shared)
        m["xp"] = f32(inp["x_prompt"][c])
        m["xsm"] = f32(inp["x_sample"][sl, 0])
        m["sca"] = f32(inp["state_conv_a"][0, sl])
        m["scb"] = f32(inp["state_conv_b"][0, sl])
        m["sss"] = f32(inp["state_ssm"][0, sl].reshape(16, 1024, 128))
        in_maps.append(m)
    res = run_bass_kernel_spmd(P.nc, in_maps, core_ids=list(range(_cores)))
    R = res.results
    if _taps:
        kernel.last_taps = [{k: r["dbg_" + k] for k in P.tap_shapes} for r in R]
    n = _cores
    y_prompt = np.stack([R[c]["yp"] for c in range(n)])
    y_sample = np.concatenate([R[c]["ysm"] for c in range(n)])[:, None, :]
    nca_p = np.stack([R[c]["nca_p"] for c in range(n)])[None]
    ncb_p = np.stack([R[c]["ncb_p"] for c in range(n)])[None]
    nss_p = np.stack([R[c]["nss_p"].reshape(16, 64, 128) for c in range(n)])[None]
    nca_s = np.concatenate([R[c]["nca_s"] for c in range(n)])[None]
    ncb_s = np.concatenate([R[c]["ncb_s"] for c in range(n)])[None]
    nss_s = np.concatenate([R[c]["nss_s"].reshape(16, 16, 64, 128) for c in range(n)])[None]
    return tuple(np.ascontiguousarray(a, dtype=np.float32) for a in
                 (y_prompt, y_sample, nca_p, ncb_p, nss_p, nca_s, ncb_s, nss_s))
```
